# Optimizing a Trainium2 kernel written in Bass

```python
import math
import jax
import jax.numpy as jnp
from jax import lax
import numpy as np


D_MODEL = 1024
BATCH = 2
SEQ = 8192
DEPTH = 2

CHUNK = 64
Q_BLOCK = 128
NORM_EPS = 1e-6

MLA_HEADS = 4
MLA_Q_RANK = 256
MLA_KV_RANK = 128
MLA_NOPE = 128
MLA_ROPE = 64
MLA_V = 128
MLA_WIDTH = MLA_HEADS * MLA_V
ROPE_THETA = 10000.0

SSD_HEADS = 8
SSD_HEAD_DIM = 64
SSD_WIDTH = SSD_HEADS * SSD_HEAD_DIM
SSD_GROUPS = 2
SSD_STATE = 128
SSD_CONV = 4
SSD_XBC = SSD_WIDTH + 2 * SSD_GROUPS * SSD_STATE

RWKV_HEADS = 8
RWKV_HEAD_DIM = 64
RWKV_WIDTH = RWKV_HEADS * RWKV_HEAD_DIM
DECAY_LORA = 64
ICLR_LORA = 64
DECAY_SCALE = 0.606531
GN_EPS = 64e-5

D_MIX = MLA_WIDTH + SSD_WIDTH + RWKV_WIDTH
IN_SPLIT_SIZES = (MLA_Q_RANK, MLA_KV_RANK, MLA_ROPE, MLA_WIDTH,
                  SSD_WIDTH, SSD_XBC, SSD_HEADS,
                  RWKV_WIDTH, RWKV_WIDTH, RWKV_WIDTH, DECAY_LORA, ICLR_LORA, RWKV_WIDTH)
D_IN = (MLA_Q_RANK + MLA_KV_RANK + MLA_ROPE + MLA_WIDTH + SSD_WIDTH + SSD_XBC + SSD_HEADS
        + 3 * RWKV_WIDTH + DECAY_LORA + ICLR_LORA + RWKV_WIDTH)

kernel_name = 'hybrid_mla_ssd_rwkv7_adaln_chunk_causal'


def rms_norm(x, w):
    xf = x.astype(jnp.float32)
    y = xf * lax.rsqrt(jnp.mean(xf * xf, axis=-1, keepdims=True) + NORM_EPS)
    return (y * w.astype(jnp.float32)).astype(x.dtype)


def split_cols(t, sizes):
    idx = np.cumsum(np.array(sizes))[:-1].tolist()
    return jnp.split(t, idx, axis=-1)


def rope_angles(positions, dim):
    half = dim // 2
    inv_freq = ROPE_THETA ** (-jnp.arange(half, dtype=jnp.float32) / half)
    ang = positions.astype(jnp.float32)[..., None] * inv_freq
    return jnp.cos(ang), jnp.sin(ang)


def apply_rope(x, cos, sin):
    x1, x2 = jnp.split(x, 2, axis=-1)
    cos = cos.astype(x.dtype)
    sin = sin.astype(x.dtype)
    return jnp.concatenate([x1 * cos - x2 * sin, x1 * sin + x2 * cos], axis=-1)


def chunk_causal_attention(q, k, v):
    b, s, h, dqk = q.shape
    n_blk = s // Q_BLOCK
    scale = dqk ** -0.5
    q_blocks = jnp.moveaxis(q.reshape(b, n_blk, Q_BLOCK, h, dqk), 1, 0)
    key_chunk = jnp.arange(s) // CHUNK

    def one_block(args):
        q_blk, blk = args
        scores = jnp.einsum('bqhd,bkhd->bhqk', q_blk, k).astype(jnp.float32) * scale
        query_chunk = (blk * Q_BLOCK + jnp.arange(Q_BLOCK)) // CHUNK
        mask = key_chunk[None, :] <= query_chunk[:, None]
        probs = jax.nn.softmax(jnp.where(mask, scores, -jnp.inf), axis=-1)
        return jnp.einsum('bhqk,bkhd->bqhd', probs.astype(v.dtype), v)

    out = lax.map(one_block, (q_blocks, jnp.arange(n_blk)))
    return jnp.moveaxis(out, 0, 1).reshape(b, s, h, v.shape[-1])


def mla_branch(q_lat, kv_lat, k_pe, cos, sin, q_norm_w, w_uq, kv_norm_w, w_ukv):
    b, s, _ = q_lat.shape
    q = (rms_norm(q_lat, q_norm_w) @ w_uq).reshape(b, s, MLA_HEADS, MLA_NOPE + MLA_ROPE)
    q_nope, q_pe = jnp.split(q, [MLA_NOPE], axis=-1)
    q_pe = apply_rope(q_pe, cos[:, :, None, :], sin[:, :, None, :])
    kv = (rms_norm(kv_lat, kv_norm_w) @ w_ukv).reshape(b, s, MLA_HEADS, MLA_NOPE + MLA_V)
    k_nope, v = jnp.split(kv, [MLA_NOPE], axis=-1)
    k_pe = apply_rope(k_pe, cos, sin)
    k_pe = jnp.broadcast_to(k_pe[:, :, None, :], (b, s, MLA_HEADS, MLA_ROPE))
    q = jnp.concatenate([q_nope, q_pe], axis=-1)
    k = jnp.concatenate([k_nope, k_pe], axis=-1)
    o = chunk_causal_attention(q, k, v)
    return o.reshape(b, s, MLA_WIDTH)


def causal_depthwise_conv(u, w, bias):
    out = lax.conv_general_dilated(
        u, w[:, None, :], window_strides=(1,), padding=[(w.shape[0] - 1, 0)],
        dimension_numbers=('NWC', 'WIO', 'NWC'), feature_group_count=u.shape[-1])
    return out + bias


def ssd_chunked(xs, dt, a, bm, cm):
    b, s, h, p = xs.shape
    g, n = bm.shape[2], bm.shape[3]
    r = h // g
    nc, L = s // CHUNK, CHUNK
    xdt = (xs * dt[..., None]).reshape(b, nc, L, g, r, p)
    a_cum = jnp.cumsum((dt * a).reshape(b, nc, L, g, r), axis=2)
    bg = bm.reshape(b, nc, L, g, n)
    cg = cm.reshape(b, nc, L, g, n)
    causal = jnp.tril(jnp.ones((L, L), dtype=bool))[None, None, :, :, None, None]
    seg = a_cum[:, :, :, None] - a_cum[:, :, None, :]
    decay = jnp.exp(jnp.where(causal, seg, -jnp.inf))
    cb = jnp.einsum('bctgn,bcsgn->bctsg', cg, bg)
    y_diag = jnp.einsum('bctsg,bctsgr,bcsgrp->bctgrp', cb, decay, xdt)
    decay_to_end = jnp.exp(a_cum[:, :, -1:] - a_cum)
    states = jnp.einsum('bcsgn,bcsgr,bcsgrp->bcgrpn', bg, decay_to_end, xdt)
    chunk_decay = jnp.exp(a_cum[:, :, -1])

    def carry_state(state, inp):
        st, dec = inp
        return state * dec[..., None, None] + st, state

    init = jnp.zeros((b, g, r, p, n), jnp.float32)
    _, prev = lax.scan(carry_state, init, (jnp.moveaxis(states, 1, 0), jnp.moveaxis(chunk_decay, 1, 0)))
    prev = jnp.moveaxis(prev, 0, 1)
    y_off = jnp.einsum('bctgn,bcgrpn,bctgr->bctgrp', cg, prev, jnp.exp(a_cum))
    return (y_diag + y_off).reshape(b, s, h, p)


def ssd_branch(z, xbc, dt_raw, conv_w, conv_b, dt_bias, a_log, d_skip, ssd_norm_w):
    f32 = jnp.float32
    b, s, _ = z.shape
    xbc = jax.nn.silu(causal_depthwise_conv(xbc, conv_w, conv_b))
    xs, bm, cm = split_cols(xbc, (SSD_WIDTH, SSD_GROUPS * SSD_STATE, SSD_GROUPS * SSD_STATE))
    xs = xs.reshape(b, s, SSD_HEADS, SSD_HEAD_DIM).astype(f32)
    bm = bm.reshape(b, s, SSD_GROUPS, SSD_STATE).astype(f32)
    cm = cm.reshape(b, s, SSD_GROUPS, SSD_STATE).astype(f32)
    dt = jax.nn.softplus((dt_raw + dt_bias).astype(f32))
    a = -jnp.exp(a_log.astype(f32))
    y = ssd_chunked(xs, dt, a, bm, cm) + d_skip.astype(f32)[:, None] * xs
    y = y.reshape(b, s, SSD_WIDTH).astype(z.dtype) * jax.nn.silu(z)
    return rms_norm(y, ssd_norm_w)


def token_shift(f, mu):
    prev = jnp.pad(f, ((0, 0), (1, 0), (0, 0)))[:, :-1]
    return f + (prev - f) * mu


def wkv7_scan(r, w, k, v, a, bb):
    def step(state, inp):
        r_t, w_t, k_t, v_t, a_t, b_t = inp
        sa = jnp.einsum('bhij,bhj->bhi', state, a_t)
        state = (state * w_t[:, :, None, :] + sa[..., None] * b_t[:, :, None, :]
                 + v_t[..., None] * k_t[:, :, None, :])
        return state, jnp.einsum('bhij,bhj->bhi', state, r_t)

    bsz, s, h, n = r.shape
    init = jnp.zeros((bsz, h, n, n), jnp.float32)
    xs = tuple(jnp.moveaxis(t, 1, 0) for t in (r, w, k, v, a, bb))
    _, y = lax.scan(step, init, xs)
    return jnp.moveaxis(y, 0, 1)


def rwkv7_branch(r, k, v, w_lo, a_lo, mu_rkv, mu_w, mu_a, w0, w_lora_b, a0, a_lora_b,
                 k_k, k_a, r_k, lnx_w, lnx_b):
    f32 = jnp.float32
    b, s, _ = r.shape
    out_dtype = r.dtype
    r = token_shift(r, mu_rkv[0])
    k = token_shift(k, mu_rkv[1])
    v = token_shift(v, mu_rkv[2])
    w_lo = token_shift(w_lo, mu_w)
    a_lo = token_shift(a_lo, mu_a)
    w = jnp.exp(-DECAY_SCALE * jax.nn.sigmoid((w0 + jnp.tanh(w_lo) @ w_lora_b).astype(f32)))
    a = jax.nn.sigmoid((a0 + a_lo @ a_lora_b).astype(f32))

    def heads(t):
        return t.astype(f32).reshape(t.shape[:-1] + (RWKV_HEADS, RWKV_HEAD_DIM))

    r, k, v, w, a = heads(r), heads(k), heads(v), heads(w), heads(a)
    kk = k * heads(k_k)
    kk = kk / jnp.maximum(jnp.sqrt(jnp.sum(kk * kk, axis=-1, keepdims=True)), 1e-12)
    k = k * (1.0 + (a - 1.0) * heads(k_a))
    y = wkv7_scan(r, w, k, v, -kk, kk * a)
    mean = jnp.mean(y, axis=-1, keepdims=True)
    var = jnp.mean(jnp.square(y - mean), axis=-1, keepdims=True)
    y = (y - mean) * lax.rsqrt(var + GN_EPS) * heads(lnx_w) + heads(lnx_b)
    y = y + jnp.sum(r * k * r_k.astype(f32), axis=-1, keepdims=True) * v
    return y.reshape(b, s, RWKV_WIDTH).astype(out_dtype)


def setup_inputs(seed: int = 0) -> dict:
    key = jax.random.key(seed)
    ks = list(jax.random.split(key, 40))
    f32 = jnp.float32

    def normal(shape, scale):
        return scale * jax.random.normal(ks.pop(), shape, f32)

    def gain(shape):
        return 1.0 + 0.02 * jax.random.normal(ks.pop(), shape, f32)

    def uniform(shape, lo, hi):
        return jax.random.uniform(ks.pop(), shape, f32, lo, hi)

    x = normal((BATCH, SEQ, D_MODEL), 1.0)
    c = normal((BATCH, D_MODEL), 1.0)
    offsets = jax.random.randint(ks.pop(), (BATCH, 1), 0, 4096, dtype=jnp.int32)
    positions = jnp.arange(SEQ, dtype=jnp.int32)[None, :] + offsets
    dt0 = jnp.exp(uniform((DEPTH, SSD_HEADS), math.log(1e-3), math.log(1e-1)))
    return {
        'x': x,
        'c': c,
        'positions': positions,
        'ada_w': normal((DEPTH, D_MODEL, 3 * D_MODEL), 0.5 * D_MODEL ** -0.5),
        'ada_b': normal((DEPTH, 3 * D_MODEL), 0.02),
        'norm_w': gain((DEPTH, D_MODEL)),
        'w_in': normal((DEPTH, D_MODEL, D_IN), D_MODEL ** -0.5),
        'q_norm_w': gain((DEPTH, MLA_Q_RANK)),
        'w_uq': normal((DEPTH, MLA_Q_RANK, MLA_HEADS * (MLA_NOPE + MLA_ROPE)), MLA_Q_RANK ** -0.5),
        'kv_norm_w': gain((DEPTH, MLA_KV_RANK)),
        'w_ukv': normal((DEPTH, MLA_KV_RANK, MLA_HEADS * (MLA_NOPE + MLA_V)), MLA_KV_RANK ** -0.5),
        'conv_w': normal((DEPTH, SSD_CONV, SSD_XBC), SSD_CONV ** -0.5),
        'conv_b': normal((DEPTH, SSD_XBC), 0.02),
        'dt_bias': dt0 + jnp.log(-jnp.expm1(-dt0)),
        'a_log': jnp.log(uniform((DEPTH, SSD_HEADS), 1.0, 16.0)),
        'd_skip': 1.0 + normal((DEPTH, SSD_HEADS), 0.1),
        'ssd_norm_w': gain((DEPTH, SSD_WIDTH)),
        'mu_rkv': uniform((DEPTH, 3, RWKV_WIDTH), 0.0, 1.0),
        'mu_w': uniform((DEPTH, DECAY_LORA), 0.0, 1.0),
        'mu_a': uniform((DEPTH, ICLR_LORA), 0.0, 1.0),
        'w0': uniform((DEPTH, RWKV_WIDTH), -4.0, 1.0),
        'w_lora_b': normal((DEPTH, DECAY_LORA, RWKV_WIDTH), 0.1),
        'a0': normal((DEPTH, RWKV_WIDTH), 0.5),
        'a_lora_b': normal((DEPTH, ICLR_LORA, RWKV_WIDTH), 0.5 * ICLR_LORA ** -0.5),
        'k_k': 0.85 + normal((DEPTH, RWKV_WIDTH), 0.05),
        'k_a': 1.0 + normal((DEPTH, RWKV_WIDTH), 0.05),
        'r_k': normal((DEPTH, RWKV_HEADS, RWKV_HEAD_DIM), 0.1),
        'lnx_w': gain((DEPTH, RWKV_WIDTH)),
        'lnx_b': normal((DEPTH, RWKV_WIDTH), 0.02),
        'w_out': normal((DEPTH, D_MIX, D_MODEL), D_MIX ** -0.5),
        'final_norm_w': gain((D_MODEL,)),
    }


def reference(x, c, positions, ada_w, ada_b, norm_w, w_in, q_norm_w, w_uq, kv_norm_w, w_ukv,
              conv_w, conv_b, dt_bias, a_log, d_skip, ssd_norm_w, mu_rkv, mu_w, mu_a, w0,
              w_lora_b, a0, a_lora_b, k_k, k_a, r_k, lnx_w, lnx_b, w_out, final_norm_w):
    cos, sin = rope_angles(positions, MLA_ROPE)
    c_act = jax.nn.silu(c)
    for l in range(DEPTH):
        mod = c_act @ ada_w[l] + ada_b[l]
        shift, scale, gate = jnp.split(mod[:, None, :], 3, axis=-1)
        h = rms_norm(x, norm_w[l]) * (1.0 + scale) + shift
        (q_lat, kv_lat, k_pe, g_mla, z, xbc, dt_raw,
         r, k, v, w_lo, a_lo, g_rwkv) = split_cols(h @ w_in[l], IN_SPLIT_SIZES)
        y_mla = mla_branch(q_lat, kv_lat, k_pe, cos, sin, q_norm_w[l], w_uq[l],
                           kv_norm_w[l], w_ukv[l]) * jax.nn.silu(g_mla)
        y_ssd = ssd_branch(z, xbc, dt_raw, conv_w[l], conv_b[l], dt_bias[l], a_log[l],
                           d_skip[l], ssd_norm_w[l])
        y_rwkv = rwkv7_branch(r, k, v, w_lo, a_lo, mu_rkv[l], mu_w[l], mu_a[l], w0[l],
                              w_lora_b[l], a0[l], a_lora_b[l], k_k[l], k_a[l], r_k[l],
                              lnx_w[l], lnx_b[l]) * jax.nn.silu(g_rwkv)
        y = jnp.concatenate([y_mla, y_ssd, y_rwkv], axis=-1)
        x = x + gate * (y @ w_out[l])
    return rms_norm(x, final_norm_w)
```

```python
import numpy as np
import concourse.bass as bass
import concourse.mybir as mybir
from concourse.bass_utils import run_bass_kernel_spmd

F32 = mybir.dt.float32
BF16 = mybir.dt.bfloat16
I32 = mybir.dt.int32
ALU = mybir.AluOpType
AF = mybir.ActivationFunctionType
AX = mybir.AxisListType


class Dep:
    __slots__ = ("name", "lw", "rd", "sem", "cnt", "psum", "pend", "sem2", "cnt2")

    def __init__(self, name=""):
        self.name = name
        self.psum = False
        self.pend = None
        self.lw = None
        self.rd = {}
        self.sem = None
        self.cnt = 0
        self.sem2 = None
        self.cnt2 = 0


class View:
    __slots__ = ("ap", "dep")

    def __init__(self, ap, dep):
        self.ap = ap
        self.dep = dep

    def __getitem__(self, idx):
        return View(self.ap[idx], self.dep)

    def re(self, pat, **kw):
        return View(self.ap.rearrange(pat, **kw), self.dep)

    def bc(self, shape):
        return View(self.ap.broadcast_to(shape), self.dep)


class Tile:
    def __init__(self, h, name):
        self.h = h
        self.name = name
        self.d = Dep(name)
        self.parts = {}

    def __getitem__(self, idx):
        return View(self.h[idx], self.d)

    def k(self, key):
        d = self.parts.get(key)
        if d is None:
            d = self.parts[key] = Dep(f"{self.name}.{key}")
        return View(self.h, d)

    def v(self):
        return View(self.h[:], self.d)


ENGS = ("pe", "act", "dve", "pool", "sp")


class Prog:
    def __init__(self, same_engine_sync=True):
        self.nc = bass.Bass("TRN2", target_bir_lowering=False)
        nc = self.nc
        self.q = {e: [] for e in ENGS}
        self.cnt = {e: 0 for e in ENGS}
        self.seen = {e: {} for e in ENGS}
        self.sems = {}
        for e in ("pe", "act", "dve", "pool"):
            self.sems[e] = nc.alloc_semaphore(f"prog_{e}")
        self.ndma = 0
        self.dma_final = {}
        self.same = same_engine_sync
        self.relax_waw = True
        import os
        self.psum_guard = bool(int(os.environ.get('KGUARD', '0')))
        self.n_sb = 0
        self.banks = []
        self.bank_i = 0
        self.ninst = 0

    def sb(self, shape, dtype=F32, name=None):
        self.n_sb += 1
        name = "s_" + (name or f"sb{self.n_sb}")
        return Tile(self.nc.alloc_sbuf_tensor(name, list(shape), dtype), name)

    def make_banks(self, n=8):
        for i in range(n):
            h = self.nc.alloc_psum_tensor(f"bank{i}", [128, 512], F32)
            t = Tile(h, f"bank{i}")
            t.d.psum = True
            self.banks.append(t)
        self.junk = self.sb([128, 4], F32, "junk")

    def bank(self):
        b = self.banks[self.bank_i % getattr(self, "gen_banks", len(self.banks))]
        self.bank_i += 1
        return b

    def dram(self, name, shape, dtype=F32, kind="Internal"):
        h = self.nc.dram_tensor(name, list(shape), dtype, kind=kind)
        return Tile(h.ap() if hasattr(h, "ap") else h, name)

    def _waits(self, eng, outs, ins):
        w = {}
        seen = self.seen[eng]

        def need(ev):
            if ev is None:
                return
            k, v = ev
            if k == eng and (eng == "pe" or not self.same):
                return
            if seen.get(k, 0) >= v:
                return
            if w.get(k, 0) < v:
                w[k] = v

        for x in ins:
            need(x.dep.lw)
            if x.dep.pend:
                for ev in x.dep.pend.items():
                    need(ev)
            if x.dep.psum:
                for k, v in x.dep.rd.items():
                    if k != eng:
                        need((k, v))
        for x in outs:
            if not (self.relax_waw and x.dep.lw is not None and x.dep.lw[0] == eng):
                need(x.dep.lw)
            for k, v in x.dep.rd.items():
                need((k, v))
        for k, v in w.items():
            seen[k] = v
        return list(w.items())

    def op(self, eng, fn, outs, ins):
        waits = self._waits(eng, outs, ins)
        self.cnt[eng] += 1
        seq = self.cnt[eng]
        self.q[eng].append((waits, fn, (eng, 1)))
        for x in ins:
            x.dep.rd[eng] = seq
        for x in outs:
            x.dep.lw = (eng, seq)
            x.dep.rd = {}
        self.ninst += 1
        if self.psum_guard and eng in ("act", "dve") and any(x.dep.psum for x in ins):
            self.cnt[eng] += 1
            seq2 = self.cnt[eng]
            w2 = []
            if self.seen[eng].get(eng, 0) < seq:
                w2 = [(eng, seq)]
                self.seen[eng][eng] = seq
            j = self.junk.h
            if eng == "act":
                fn2 = lambda e: e.activation(j[0:1, 1:2], j[0:1, 0:1], AF.Identity)
            else:
                fn2 = lambda e: e.tensor_copy(j[0:1, 3:4], j[0:1, 2:3])
            self.q[eng].append((w2, fn2, (eng, 1)))
            for x in ins:
                if x.dep.psum:
                    x.dep.rd[eng] = seq2
            self.ninst += 1

    def dma(self, eng, out, in_, semdep=None, **kw):
        waits = self._waits(eng, [out], [in_])
        d = semdep or (in_.dep if eng == "pool" else out.dep)
        if eng == "pool":
            if d.sem2 is None:
                self.ndma += 1
                d.sem2 = f"dmasw{self.ndma}"
                self.sems[d.sem2] = self.nc.alloc_semaphore(d.sem2)
            d.cnt2 += 16
            skey, sval = d.sem2, d.cnt2
        else:
            if d.sem is None:
                self.ndma += 1
                d.sem = f"dma{self.ndma}"
                self.sems[d.sem] = self.nc.alloc_semaphore(d.sem)
            d.cnt += 16
            skey, sval = d.sem, d.cnt
        ev = (skey, sval)
        self.dma_final[skey] = sval
        oa, ia = out.ap, in_.ap
        self.q[eng].append((waits, lambda e: e.dma_start(out=oa, in_=ia, **kw), (skey, 16)))
        in_.dep.rd[skey] = sval
        if (semdep is not None and semdep is not out.dep) or eng == "pool":
            if out.dep.pend is None:
                out.dep.pend = {}
            out.dep.pend[skey] = sval
        else:
            out.dep.lw = ev
            out.dep.rd = {}
        self.ninst += 1

    def allreduce(self, out, in_, groups, eng="pool"):
        return self.allgather(out, in_, groups, eng=eng, kind="AllReduce", op=mybir.AluOpType.add)

    def allgather(self, out, in_, groups, eng="pool", kind="AllGather", op=mybir.AluOpType.bypass):
        waits = self._waits(eng, [out], [in_])
        d = out.dep
        if d.sem is None:
            self.ndma += 1
            d.sem = f"dma{self.ndma}"
            self.sems[d.sem] = self.nc.alloc_semaphore(d.sem)
        import os
        inc = int(os.environ.get("KCCINC", "1"))
        d.cnt += inc
        self.dma_final[d.sem] = d.cnt
        oa, ia = out.ap.opt(), in_.ap.opt()
        self.q[eng].append((waits, lambda e: e.collective_compute(
            kind, op, replica_groups=groups, ins=[ia], outs=[oa]), (d.sem, inc)))
        in_.dep.rd[d.sem] = d.cnt
        out.dep.lw = (d.sem, d.cnt)
        out.dep.rd = {}
        self.ninst += 1

    def finish(self, final_eng="sp"):
        nc = self.nc
        fin = []
        for k, v in self.dma_final.items():
            fin.append((k, v))
        for e in ("pe", "act", "dve", "pool"):
            if self.cnt[e]:
                fin.append((e, self.cnt[e]))
        self.q[final_eng].append((fin, None, None))
        engobj = {"pe": "tensor", "act": "scalar", "dve": "vector", "pool": "gpsimd", "sp": "sync"}
        sems = self.sems
        with nc.Block() as block:
            for ename in ENGS:
                lst = self.q[ename]
                if not lst:
                    continue

                def body(e, lst=lst):
                    for waits, fn, inc in lst:
                        for k, v in waits:
                            e.wait_ge(sems[k], v)
                        if fn is not None:
                            ins = fn(e)
                            ins.then_inc(sems[inc[0]], inc[1])

                getattr(block, engobj[ename])(body)
        return nc

    def mm(self, out, lhsT, rhs, start=True, stop=True):
        o, l, r = out.ap, lhsT.ap, rhs.ap
        self.op("pe", lambda e: e.matmul(o, l, r, start=start, stop=stop), [out], [lhsT, rhs])

    def tr(self, out, in_, ident):
        o, i, d = out.ap, in_.ap, ident.ap
        self.op("pe", lambda e: e.transpose(o, i, d), [out], [in_, ident])

    def act(self, out, in_, func, bias=None, scale=None, accum=None, eng="act"):
        kw = {}
        ins = [in_]
        outs = [out]
        if bias is not None:
            if isinstance(bias, View):
                kw["bias"] = bias.ap
                ins.append(bias)
            else:
                kw["bias"] = float(bias)
        if scale is not None:
            if isinstance(scale, View):
                kw["scale"] = scale.ap
                ins.append(scale)
            else:
                kw["scale"] = float(scale)
        if accum is not None:
            kw["accum_out"] = accum.ap
            outs.append(accum)
        o, i = out.ap, in_.ap
        if not hasattr(self, "actlog"):
            self.actlog = []
        self.actlog.append(str(func).split(".")[-1])
        self.op(eng, lambda e: e.activation(o, i, func, **kw), outs, ins)

    def tt(self, out, a, b, op, eng="dve"):
        o, x, y = out.ap, a.ap, b.ap
        self.op(eng, lambda e: e.tensor_tensor(o, x, y, op), [out], [a, b])

    def ts(self, out, a, s1, op0, s2=None, op1=None, eng="dve", accum=None):
        ins = [a]
        outs = [out]
        v1 = s1.ap if isinstance(s1, View) else float(s1)
        if isinstance(s1, View):
            ins.append(s1)
        v2 = None
        if s2 is not None:
            v2 = s2.ap if isinstance(s2, View) else float(s2)
            if isinstance(s2, View):
                ins.append(s2)
        o, x = out.ap, a.ap
        kw = {}
        if op1 is not None:
            kw["op1"] = op1
        if accum is not None:
            kw["accum_out"] = accum.ap
            outs.append(accum)
        self.op(eng, lambda e: e.tensor_scalar(o, x, v1, v2, op0, **kw), outs, ins)

    def stt(self, out, a, s, b, op0, op1, eng="dve", accum=None):
        ins = [a, b]
        outs = [out]
        sv = s.ap if isinstance(s, View) else float(s)
        if isinstance(s, View):
            ins.append(s)
        o, x, y = out.ap, a.ap, b.ap
        kw = {}
        if accum is not None:
            kw["accum_out"] = accum.ap
            outs.append(accum)
        self.op(eng, lambda e: e.scalar_tensor_tensor(o, x, sv, y, op0, op1, **kw), outs, ins)

    def cp(self, out, in_, eng="dve"):
        o, i = out.ap, in_.ap
        if eng == "act":
            self.op(eng, lambda e: e.activation(o, i, AF.Identity), [out], [in_])
        else:
            self.op(eng, lambda e: e.tensor_copy(o, i), [out], [in_])

    def memset(self, out, val, eng="pool"):
        o = out.ap
        self.op(eng, lambda e: e.memset(o, val), [out], [])

    def scan(self, out, d0, d1, init, op0, op1, eng="dve"):
        o, a, b = out.ap, d0.ap, d1.ap
        ins = [d0, d1]
        iv = init.ap if isinstance(init, View) else float(init)
        if isinstance(init, View):
            ins.append(init)
        self.op(eng, lambda e: e.tensor_tensor_scan(o, a, b, iv, op0, op1), [out], ins)

    def reduce(self, out, in_, op, axis=AX.X, eng="dve"):
        o, i = out.ap, in_.ap
        self.op(eng, lambda e: e.tensor_reduce(o, i, axis, op), [out], [in_])

    def recip(self, out, in_):
        o, i = out.ap, in_.ap
        self.op("dve", lambda e: e.reciprocal(o, i), [out], [in_])


import ml_dtypes

D = 1024
TT = 512
NG = 17
EPS = 1e-6
GN_EPS = 64e-5
DECAY_SCALE = 0.606531
QSCALE = 192.0 ** -0.5
ST_ENG = "pool"
import os as _os
KSUB = int(_os.environ.get('KSUB', '99'))
KSUB2 = int(_os.environ.get('KSUB2', '99'))
KDBG = int(_os.environ.get('KDBG', '0'))
KCH = int(_os.environ.get('KCH', '6'))
KATT = int(_os.environ.get('KATT', '-1'))
KKT = int(_os.environ.get('KKT', '-1'))
KSKIP = int(_os.environ.get('KSKIP', '0'))
KSQ = int(_os.environ.get('KSQ', '1'))

_CST_ITEMS = [("ident", 128), ("ones", 128), ("tri", 128), ("mst", 128), ("mit", 128), ("ms", 128),
              ("bd", 128), ("half", 128), ("hind", 2), ("rm128", 512), ("rm64", 512), ("invf", 1), ("sgn", 1)]
CST = {}
_o = 0
for _n, _w in _CST_ITEMS:
    CST[_n] = (_o, _w)
    _o += _w
NCST = _o

_PC_ITEMS = [("normw", 8), ("shiftb", 8), ("scaleb", 8), ("cvec", 8), ("qnw", 2), ("kvnw", 1),
             ("cw_xs", 4), ("cw_B", 4), ("cw_C", 4), ("cb_xs", 1), ("cb_B", 1), ("cb_C", 1),
             ("ssdnw", 1), ("dskip", 1), ("dtb0", 1), ("dtb1", 1), ("alog0", 1), ("alog1", 1),
             ("mu_r", 1), ("mu_k", 1), ("mu_v", 1), ("mu_lo", 1), ("w0", 1), ("a0", 1), ("kk", 1),
             ("ka", 1), ("rk", 1)]
PC = {}
_o = 0
for _n, _w in _PC_ITEMS:
    PC[_n] = (_o, _w)
    _o += _w
NPC = _o


def consts_array():
    c = np.zeros((128, NCST), np.float32)
    p = np.arange(128)[:, None]
    f = np.arange(128)[None, :]

    def put(name, a):
        o, w = CST[name]
        c[:, o:o + w] = a

    put("ident", (p == f))
    put("ones", np.ones((128, 128)))
    put("tri", (f >= p))
    same = (p // 64) == (f // 64)
    put("mst", same & (p < f))
    put("mit", same & (p <= f))
    put("ms", same & (f < p))
    put("bd", same)
    put("half", np.full((128, 128), 0.5))
    put("hind", (p // 64) == np.arange(2)[None, :])
    cc = np.arange(512)[None, :]
    put("rm128", np.broadcast_to((cc % 128 != 0), (128, 512)))
    put("rm64", np.broadcast_to((cc % 64 != 0), (128, 512)))
    invf = (10000.0 ** (-(np.arange(32, dtype=np.float32)) / 32.0)).astype(np.float32)
    iv = np.concatenate([invf, invf, invf, invf]).reshape(128, 1)
    put("invf", iv)
    sg = np.ones((128, 1), np.float32)
    sg[0:32] = -1.0
    sg[64:96] = -1.0
    put("sgn", sg)
    return c


O_QLAT, O_KVLAT, O_KPE, O_GMLA, O_Z, O_XS, O_B, O_C, O_DT = 0, 256, 384, 448, 960, 1472, 1984, 2240, 2496
O_R, O_K, O_V, O_WLO, O_ALO, O_GRW = 2504, 3016, 3528, 4040, 4104, 4168


def _col8(v):
    return np.ascontiguousarray(np.asarray(v, np.float32).reshape(8, 128).T)


def prep_A(inp, l, b, g, S):
    f32 = np.float32
    w_in = inp["w_in"][l]
    grp = g // 2
    cols = []
    cols.append(np.arange(O_QLAT, O_QLAT + 128))
    cols.append(np.arange(O_QLAT + 128, O_QLAT + 256))
    cols.append(np.arange(O_KVLAT, O_KVLAT + 128))
    kpe = np.arange(O_KPE, O_KPE + 64)
    cols.append(np.concatenate([kpe, kpe]))
    cols.append(np.arange(O_GMLA + 128 * g, O_GMLA + 128 * g + 128))
    cols.append(np.arange(O_Z + 128 * g, O_Z + 128 * g + 128))
    cols.append(np.arange(O_XS + 128 * g, O_XS + 128 * g + 128))
    cols.append(np.arange(O_B + 128 * grp, O_B + 128 * grp + 128))
    cols.append(np.arange(O_C + 128 * grp, O_C + 128 * grp + 128))
    cols.append(np.full(128, O_DT + 2 * g))
    cols.append(np.full(128, O_DT + 2 * g + 1))
    cols.append(np.arange(O_R + 128 * g, O_R + 128 * g + 128))
    cols.append(np.arange(O_K + 128 * g, O_K + 128 * g + 128))
    cols.append(np.arange(O_V + 128 * g, O_V + 128 * g + 128))
    cols.append(np.concatenate([np.arange(O_WLO, O_WLO + 64), np.arange(O_ALO, O_ALO + 64)]))
    cols.append(np.arange(O_GRW + 128 * g, O_GRW + 128 * g + 128))
    cols.append(np.concatenate([kpe[32:], kpe[:32], kpe[32:], kpe[:32]]))
    cols = np.concatenate(cols)
    wcat = np.ascontiguousarray(w_in[:, cols])

    pc = np.zeros((128, NPC), f32)

    def put(name, a):
        o, w = PC[name]
        pc[:, o:o + w] = np.asarray(a, f32).reshape(128, w)

    put("normw", _col8(inp["norm_w"][l]))
    put("shiftb", _col8(inp["ada_b"][l][0:1024]))
    put("scaleb", _col8(inp["ada_b"][l][1024:2048]))
    put("cvec", _col8(inp["c"][b]))
    put("qnw", np.asarray(inp["q_norm_w"][l]).reshape(2, 128).T)
    put("kvnw", inp["kv_norm_w"][l])
    cw = inp["conv_w"][l]
    cb = inp["conv_b"][l]
    sl_xs = slice(128 * g, 128 * g + 128)
    sl_B = slice(512 + 128 * grp, 512 + 128 * grp + 128)
    sl_C = slice(768 + 128 * grp, 768 + 128 * grp + 128)
    put("cw_xs", cw[:, sl_xs].T)
    put("cw_B", cw[:, sl_B].T)
    put("cw_C", cw[:, sl_C].T)
    put("cb_xs", cb[sl_xs])
    put("cb_B", cb[sl_B])
    put("cb_C", cb[sl_C])
    put("ssdnw", inp["ssd_norm_w"][l][128 * g:128 * g + 128])
    put("dskip", np.repeat(inp["d_skip"][l][2 * g:2 * g + 2], 64))
    put("dtb0", np.full(128, inp["dt_bias"][l][2 * g]))
    put("dtb1", np.full(128, inp["dt_bias"][l][2 * g + 1]))
    put("alog0", np.full(128, inp["a_log"][l][2 * g]))
    put("alog1", np.full(128, inp["a_log"][l][2 * g + 1]))
    hs = slice(128 * g, 128 * g + 128)
    put("mu_r", inp["mu_rkv"][l][0][hs])
    put("mu_k", inp["mu_rkv"][l][1][hs])
    put("mu_v", inp["mu_rkv"][l][2][hs])
    put("mu_lo", np.concatenate([inp["mu_w"][l], inp["mu_a"][l]]))
    put("w0", inp["w0"][l][hs])
    put("a0", inp["a0"][l][hs])
    put("kk", inp["k_k"][l][hs])
    put("ka", inp["k_a"][l][hs])
    put("rk", np.asarray(inp["r_k"][l]).reshape(512)[hs])

    wuq = inp["w_uq"][l][:, 192 * g:192 * g + 192]
    pe = wuq[:, 128:192]
    pes = np.concatenate([pe[:, 32:], pe[:, :32]], axis=1)
    wuq_c = np.concatenate([wuq[:, 0:128], pe, pe, pes, pes], axis=1)
    wukv_c = inp["w_ukv"][l][:, 256 * g:256 * g + 256]
    z64 = np.zeros((64, 128), np.float32)
    lora = np.concatenate([np.concatenate([inp["w_lora_b"][l][:, hs], z64], axis=0),
                           np.concatenate([z64, inp["a_lora_b"][l][:, hs]], axis=0)], axis=1)
    lnrow = np.concatenate([np.broadcast_to(inp["lnx_w"][l][hs], (128, 128)),
                            np.broadcast_to(inp["lnx_b"][l][hs], (128, 128))], axis=1)
    return {
        "wcat": wcat.astype(f32),
        "adaw": np.ascontiguousarray(inp["ada_w"][l][:, 0:2048]).astype(f32),
        "pcol": pc,
        "wuq": np.ascontiguousarray(wuq_c).astype(f32),
        "wukv": np.ascontiguousarray(wukv_c).astype(f32),
        "lora": np.ascontiguousarray(lora).astype(f32),
        "lnrow": np.ascontiguousarray(lnrow).astype(f32),
        "pos": np.ascontiguousarray(inp["positions"][b][None, :S]).astype(np.int32),
        "cst": consts_array(),
    }


class PhaseA:
    def __init__(self, P, S, x, y_loc, ssq_out, io, tag=""):
        self.P = P
        self.S = S
        self.x = x
        self.y_loc = y_loc
        self.ssq_out = ssq_out
        self.io = io
        self.tag = tag

    def c(self, name, rows=slice(0, 128)):
        o, w = CST[name]
        return self.cst[rows, o:o + w]

    def bfv(self, name):
        t = self.W[name]
        return View(t.h[:].bitcast(BF16)[:, 0:TT], t.d)

    def ydst(self, br, tok0):
        if isinstance(self.y_loc, list):
            return self.y_loc[tok0 // 1024][br * 128:(br + 1) * 128, tok0 % 1024:tok0 % 1024 + TT]
        return self.y_loc[br * 128:(br + 1) * 128, tok0:tok0 + TT]

    def pc(self, name, j=0, rows=slice(0, 128)):
        o, w = PC[name]
        return self.pcol[rows, o + j:o + j + 1]

    def setup(self):
        P, io = self.P, self.io
        S = self.S
        if not hasattr(self, "_tiles"):
            self._tiles = {}

        def sb(shape, dt=F32, name=None):
            t = self._tiles.get(name)
            if t is None:
                t = self._tiles[name] = P.sb(shape, dt, name)
            return t

        self.cst = sb([128, NCST], F32, "cst" + self.tag)
        P.dma("sp", self.cst.v(), io["cst"].v())
        self.pcol = sb([128, NPC], F32, "pcol" + self.tag)
        P.dma("sp", self.pcol.v(), io["pcol"].v())
        self.lnrow = sb([128, 256], F32, "lnrow" + self.tag)
        P.dma("sp", self.lnrow.v(), io["lnrow"].v())
        W = {}

        def wt(name, shape, dt=F32):
            W[name] = sb(shape, dt, name + self.tag)
            return W[name]

        self.xt = [wt(f"xt{i}", [128, 1024]) for i in range(2)]
        self.xn = [wt("xn0", [128, 1024])]
        stg = [self.xt[0], self.xt[1], self.xn[0]]
        NW = NG * 128
        self.w_sb = sb([128, 8, NW], BF16, "w_sb" + self.tag)
        wv = io["wcat"].v().re("(c p) n -> p c n", p=128)
        n = 0
        for kc in range(8):
            for c0 in range(0, NW, 1024):
                c1 = min(NW, c0 + 1024)
                s_ = stg[n % 3]
                n += 1
                P.dma("sp", s_[:, 0:c1 - c0], wv[:, kc, c0:c1])
                P.cp(self.w_sb[:, kc, c0:c1], s_[:, 0:c1 - c0], eng=("dve", "act")[n % 2])
        wq = self.xn[0]
        P.dma("sp", wq[:, 0:768].re("p (c n) -> p c n", c=2), io["wuq"].v().re("(c p) n -> p c n", p=128))
        self.wuq = sb([128, 2, 384], BF16, "wuq" + self.tag)
        P.cp(self.wuq.v(), wq[:, 0:768].re("p (c n) -> p c n", c=2))
        wk = self.xt[0]
        P.dma("sp", wk[:, 0:256], io["wukv"].v())
        self.wukv = sb([128, 256], BF16, "wukv" + self.tag)
        P.cp(self.wukv.v(), wk[:, 0:256])
        self.cbf = sb([128, 4, 128], BF16, "cbf" + self.tag)
        for j_, nm_ in enumerate(("ident", "ones", "half", "bd")):
            P.cp(self.cbf[:, j_, :], self.c(nm_))
        self.xnb = sb([128, 1024], BF16, "xnb" + self.tag)
        self.lora = sb([128, 256], F32, "lora" + self.tag)
        P.dma("sp", self.lora.v(), io["lora"].v())

        self.dc = sb([128, 64], F32, "dcols" + self.tag)
        dc = self.dc
        self.cact = dc[:, 0:8]
        P.act(self.cact, self.pcol[:, PC["cvec"][0]:PC["cvec"][0] + 8], AF.Silu)
        bk = P.bank()
        aw = io["adaw"].v().re("(c p) n -> p c n", p=128)
        st4 = [self.xt[0], self.xt[1]]
        for piece in range(8):
            for kc in range(8):
                dst = st4[kc // 4][:, (kc % 4) * 256:(kc % 4) * 256 + 256]
                P.dma("sp", dst, aw[:, kc, piece * 256:(piece + 1) * 256])
            for fcl in range(2):
                fc = piece * 2 + fcl
                for kc in range(8):
                    src = st4[kc // 4][:, (kc % 4) * 256 + fcl * 128:(kc % 4) * 256 + fcl * 128 + 128]
                    P.mm(bk[:, fc:fc + 1], src, self.cact[:, kc:kc + 1], start=(kc == 0), stop=(kc == 7))
        self.shift = dc[:, 8:16]
        self.Acol = dc[:, 16:24]
        o_sh, o_sc, o_nw = PC["shiftb"][0], PC["scaleb"][0], PC["normw"][0]
        P.tt(self.shift, bk[:, 0:8], self.pcol[:, o_sh:o_sh + 8], ALU.add)
        tmp8 = dc[:, 24:32]
        P.tt(tmp8, bk[:, 8:16], self.pcol[:, o_sc:o_sc + 8], ALU.add)
        P.stt(self.Acol, tmp8, 1.0, self.pcol[:, o_nw:o_nw + 8], ALU.add, ALU.mult)
        self.ommu = dc[:, 32:36]
        o_mu = PC["mu_r"][0]
        P.ts(self.ommu, self.pcol[:, o_mu:o_mu + 4], -1.0, ALU.mult, 1.0, ALU.add)
        self.omka = dc[:, 36:37]
        P.ts(self.omka, self.pc("ka"), -1.0, ALU.mult, 1.0, ALU.add)
        self.ah = dc[:, 37:39]
        o_al = PC["alog0"][0]
        P.act(dc[:, 40:42], self.pcol[:, o_al:o_al + 2], AF.Exp)
        P.ts(self.ah, dc[:, 40:42], -1.0, ALU.mult)
        self.kmax2 = dc[:, 42:43]
        P.memset(self.kmax2, 0.0)

        self.KA = sb([128, S], BF16, "KA" + self.tag)
        self.KB = sb([128, S // 2], BF16, "KB" + self.tag)
        nt = S // 128
        self.V = sb([128, nt, 132], BF16, "V" + self.tag)
        P.memset(self.V[:, :, 129:132], 0.0)
        P.memset(self.V[:, :, 128:129], 1.0)
        self.cbuf = [sb([128, 3 + TT], F32, f"cbuf{i}" + self.tag) for i in range(3)]
        for t in self.cbuf:
            P.memset(t[:, 0:3], 0.0)
        self.sbuf = [sb([128, 1 + TT], F32, f"shb{i}" + self.tag) for i in range(4)]
        for t in self.sbuf:
            P.memset(t[:, 0:1], 0.0)
        self.prev = sb([128, 128], F32, "ssd_prev" + self.tag)
        P.memset(self.prev.v(), 0.0)
        self.prev_bf = sb([128, 128], BF16, "ssd_prevbf" + self.tag)
        P.memset(self.prev_bf.v(), 0.0)
        self.Sbd = [sb([128, 128], F32, f"Sbd{i}" + self.tag) for i in range(2)]
        P.memset(self.Sbd[0].v(), 0.0)
        self.sbd_i = 0
        self.rh0 = [sb([128, 128], F32, f"rh0_{i}" + self.tag) for i in range(2)]
        self.rh1 = [sb([128, 128], F32, f"rh1_{i}" + self.tag) for i in range(2)]
        for t in self.rh0:
            P.memset(t[:, 64:128], 0.0)
        for t in self.rh1:
            P.memset(t[:, 0:64], 0.0)

        wt("stat", [128, 16])
        wt("hT", [128, 8, TT], BF16)
        G = [wt(f"G{i}", [128, TT]) for i in range(15)]
        for i in range(6):
            wt(f"t{i}", [128, TT])
        for j, n_ in enumerate(["r_s", "k_s", "v_s", "lo_s", "sg", "logw", "a_s", "kkn", "kp", "lw", "rt", "kt", "bt", "at"]):
            W[n_] = G[j]
        for j, n_ in enumerate(["sz", "xsT", "BT", "CT", "dtb0", "dtb1", "ac0", "ac1"]):
            W[n_] = G[j]
        for j, n_ in enumerate(["qlat0", "qlat1", "kvlat", "sqA", "sqB", "rstd", "gm"]):
            W[n_] = G[8 + j]
        W["CC"] = G[0]
        W["SS"] = G[1]
        W["posi"] = G[2]
        W["ki"] = G[3]
        wt("qn", [128, 2, TT], BF16)
        wt("kvn", [128, TT], BF16)
        wt("QA", [128, TT], BF16)
        wt("QB", [128, TT], BF16)
        wt("BTb", [128, TT], BF16)
        wt("CTb", [128, TT], BF16)
        self.PT = [wt(f"PT{i}", [128, TT], BF16) for i in range(3)]
        wt("yout", [128, TT], BF16)
        wt("yout2", [128, TT], BF16)
        wt("yout3", [128, TT], BF16)
        wt("ssqr", [1, TT])
        self.W = W
        self.G = G
        if not hasattr(self, "scr"):
            self.scr = {}

    def sc(self, name, shape=(128, 128), dt=F32, n=1):
        key = name
        ent = self.scr.get(key)
        if ent is None:
            ent = self.scr[key] = [[self.P.sb(list(shape), dt, f"{name}_{i}" + self.tag) for i in range(n)], 0]
        t = ent[0][ent[1] % n]
        ent[1] += 1
        return t

    def run(self):
        self.setup()
        import os
        if int(os.environ.get("KSTAGE", "9")) < 1:
            return
        self.run_tiles()

    def ht_gen(self, i):
        P, W = self.P, self.W
        tok0 = i * TT
        hT = W["hT"]
        stat = W["stat"]
        for sub in range(4):
            xt = self.xt[sub % 2]
            xn = self.xn[0]
            P.dma("sp", xt.v(), self.x[tok0 + sub * 128: tok0 + sub * 128 + 128, :])
            ssq = stat[:, sub:sub + 1]
            P.act(xn.v(), xt.v(), AF.Square, accum=ssq)
            rs = stat[:, 4 + sub:5 + sub]
            P.act(rs, ssq, AF.Sqrt, scale=1.0 / D, bias=EPS)
            rr = stat[:, 8 + sub:9 + sub]
            P.recip(rr, rs)
            xnb = self.xnb
            P.ts(xnb.v(), xt.v(), rr, ALU.mult)
            yield
            bk = P.bank()
            bkb = View(bk.h[:].bitcast(BF16), bk.d)
            identb = self.cbf[:, 0, :]
            for fc in range(8):
                P.tr(bkb[:, fc * 128:(fc + 1) * 128], xnb[:, fc * 128:(fc + 1) * 128], identb)
            for fc in range(8):
                P.act(hT[:, fc, sub * 128:(sub + 1) * 128], bkb[:, fc * 128:(fc + 1) * 128], AF.Identity,
                      bias=self.shift[:, fc:fc + 1], scale=self.Acol[:, fc:fc + 1])
            yield

    def inproj(self, gi, m0=0, m1=128):
        P = self.P
        hT = self.W["hT"]
        bk = P.bank()
        for kc in range(8):
            P.mm(bk[0:m1 - m0, :], self.w_sb[:, kc, gi * 128 + m0: gi * 128 + m1], hT[:, kc, :],
                 start=(kc == 0), stop=(kc == 7))
        return bk

    @staticmethod
    def interleave(gens):
        gens = [g for g in gens if g is not None]
        while gens:
            for g in list(gens):
                try:
                    next(g)
                except StopIteration:
                    gens.remove(g)

    def run_tiles(self):
        nt = self.S // TT
        ht_done = False
        for i in range(nt):
            if not ht_done:
                self.interleave([self.ht_gen(i)])
            self.P.gen_banks = 8
            pb = self.mla_proj(i)
            self.rope_tables(i)
            self.mla_pre(i, pb)
            self.P.gen_banks = 4
            att = self.att_gen(i)
            rest = self.rest_gen(i, nt)
            att_alive = rest_alive = True
            while att_alive or rest_alive:
                if att_alive:
                    try:
                        next(att)
                    except StopIteration:
                        att_alive = False
                        self.P.gen_banks = 8
                if rest_alive:
                    try:
                        next(rest)
                    except StopIteration:
                        rest_alive = False
            ht_done = (i + 1 < nt)

    def rest_gen(self, i, nt):
        yield from self.ssd_gen(i)
        g = self.rwkv_gen(i)
        next(g)
        yield
        if i + 1 < nt:
            gs = [g, self.ht_gen(i + 1)]
            while gs:
                for x_ in list(gs):
                    try:
                        next(x_)
                    except StopIteration:
                        gs.remove(x_)
                yield
        else:
            yield from g

    def rope_tables(self, i):
        import os
        P, W = self.P, self.W
        tok0 = i * TT
        CC, SS = W["CC"], W["SS"]
        posi = View(W["posi"].h[:].bitcast(I32), W["posi"].d)
        ki = View(W["ki"].h[:].bitcast(I32), W["ki"].d)
        t0, t1, t2 = W["t0"], W["t1"], W["t2"]
        pv = self.io["pos"]
        P.dma("sp", posi, View(pv.h[0:1, tok0:tok0 + TT].broadcast_to([128, TT]), pv.d))
        KR = int(os.environ.get("KROPE", "9")) if i == 3 else 9
        if KR < 2:
            return
        ang = t0.v()
        P.cp(ang, posi)
        P.ts(ang, ang, self.c("invf"), ALU.mult)
        TWO_PI = 6.283185307179586
        C1 = 6.28125
        C2 = TWO_PI - C1
        if KR < 3:
            return
        for which, out in ((0, SS), (1, CC)):
            a2 = t1.v()
            if which == 1:
                P.ts(a2, ang, 1.5707963267948966, ALU.add)
            else:
                P.cp(a2, ang)
            kf = t2.v()
            P.ts(kf, a2, 1.0 / TWO_PI, ALU.mult)
            if KR < 4:
                continue
            P.cp(ki, kf)
            P.cp(kf, ki)
            P.stt(a2, kf, -C1, a2, ALU.mult, ALU.add)
            P.stt(a2, kf, -C2, a2, ALU.mult, ALU.add)
            P.ts(a2, a2, 3.1415925, ALU.min, -3.1415925, ALU.max)
            if KR < 5:
                continue
            P.act(out.v(), a2, AF.Sin)
        P.ts(SS.v(), SS.v(), self.c("sgn"), ALU.mult)

    def rms_feat(self, banks, dsts, nw_cols, nfeat, out_bf):
        P, W = self.P, self.W
        ones = self.cbf[:, 1, :]
        sqs = [self.bfv("sqA"), self.bfv("sqB")]
        for j, bk in enumerate(banks):
            P.act(dsts[j].v(), bk, AF.Identity)
            P.act(sqs[j], bk, AF.Square)
        sb_ = P.bank()
        for j in range(len(banks)):
            P.mm(sb_[:, :], ones, sqs[j], start=(j == 0), stop=(j == len(banks) - 1))
        rstd = W["rstd"]
        P.act(rstd.v(), sb_[:, :], AF.Sqrt, scale=1.0 / nfeat, bias=EPS)
        P.recip(rstd.v(), rstd.v())
        for j in range(len(banks)):
            P.stt(out_bf[j], dsts[j].v(), nw_cols[j], rstd.v(), ALU.mult, ALU.mult)

    def mla_proj(self, i):
        return {g_: self.inproj(g_) for g_ in (0, 1, 2, 4, 3, 16)}

    def mla_pre(self, i, pb):
        P, W = self.P, self.W
        S = self.S
        tok0 = i * TT
        ones = self.c("ones")
        ident = self.c("ident")
        b0 = pb[0]
        b1 = pb[1]
        qn = W["qn"]
        self.rms_feat([b0[:, :], b1[:, :]], [W["qlat0"], W["qlat1"]],
                      [self.pc("qnw", 0), self.pc("qnw", 1)], 256, [qn[:, 0, :], qn[:, 1, :]])
        b2 = pb[2]
        kvn = W["kvn"]
        self.rms_feat([b2[:, :]], [W["kvlat"]], [self.pc("kvnw")], 128, [kvn.v()])
        b4 = pb[4]
        P.act(W["gm"].v(), b4[:, :], AF.Silu)
        QA, QB = W["QA"], W["QB"]
        sqA, sqB = self.bfv("sqA"), self.bfv("sqB")
        onesb, halfb = self.cbf[:, 1, :], self.cbf[:, 2, :]
        CC, SS = W["CC"], W["SS"]
        t0, t1 = W["t0"], W["t1"]
        bq = P.bank()
        for c in range(2):
            P.mm(bq[:, :], self.wuq[:, c, 0:128], qn[:, c, :], start=(c == 0), stop=(c == 1))
        P.act(QA.v(), bq[:, :], AF.Identity, scale=QSCALE)
        P.act(sqA, bq[:, :], AF.Square, scale=QSCALE)
        bp = P.bank()
        for c in range(2):
            P.mm(bp[:, :], self.wuq[:, c, 128:256], qn[:, c, :], start=(c == 0), stop=(c == 1))
        bs = P.bank()
        for c in range(2):
            P.mm(bs[:, :], self.wuq[:, c, 256:384], qn[:, c, :], start=(c == 0), stop=(c == 1))
        P.act(sqB, bp[:, :], AF.Square, scale=QSCALE)
        P.tt(t0.v(), bp[:, :], CC.v(), ALU.mult)
        P.tt(t1.v(), bs[:, :], SS.v(), ALU.mult)
        P.tt(t0.v(), t0.v(), t1.v(), ALU.add)
        P.ts(QB.v(), t0.v(), QSCALE, ALU.mult)
        bqq = P.bank()
        P.mm(bqq[:, :], onesb, sqA, start=True, stop=False)
        P.mm(bqq[:, :], halfb, sqB, start=False, stop=True)
        qmx = self.dc[:, 44:45]
        P.reduce(qmx, bqq[:, :], ALU.max)
        bk3 = pb[3]
        bk3s = pb[16]
        P.act(sqB, bk3[:, :], AF.Square)
        P.tt(t0.v(), bk3[:, :], CC.v(), ALU.mult)
        P.tt(t1.v(), bk3s[:, :], SS.v(), ALU.mult)
        hlf = 0 if tok0 < S // 2 else 1
        kr = slice(64 * hlf, 64 * hlf + 64)
        kc0 = tok0 - hlf * (S // 2)
        P.tt(self.KB[kr, kc0:kc0 + TT], t0[kr, :], t1[kr, :], ALU.add)
        bkn = P.bank()
        P.mm(bkn[:, :], self.wukv[:, 0:128], kvn.v())
        P.act(self.KA[:, tok0:tok0 + TT], bkn[:, :], AF.Identity)
        P.act(sqA, bkn[:, :], AF.Square)
        bkk = P.bank()
        P.mm(bkk[:, :], onesb, sqA, start=True, stop=False)
        P.mm(bkk[:, :], halfb, sqB, start=False, stop=True)
        kmx = self.dc[:, 43:44]
        P.reduce(kmx, bkk[:, :], ALU.max)
        P.tt(self.kmax2, self.kmax2, kmx, ALU.max)
        bv = P.bank()
        for sub in range(4):
            P.mm(bv[:, sub * 128:(sub + 1) * 128], kvn[:, sub * 128:(sub + 1) * 128], self.wukv[:, 128:256])
        P.cp(self.V[:, 4 * i:4 * i + 4, 0:128], bv[:, :].re("p (s d) -> p s d", d=128))
        nb = self.dc[:, 45:46]
        P.tt(nb, qmx, self.kmax2, ALU.mult)
        P.act(nb, nb, AF.Sqrt)
        P.ts(nb, nb, -1.0, ALU.mult)
        return

    def att_gen(self, i):
        P, W = self.P, self.W
        S = self.S
        tok0 = i * TT
        ident = self.c("ident")
        QA, QB = W["QA"], W["QB"]
        nb = self.dc[:, 45:46]
        acc = P.banks[4:8]
        nk = 4 * i + 4
        for kt in range(nk):
            j = kt - 4 * i
            q0 = 0 if j < 0 else 128 * j
            khalf = 0 if kt * 128 < S // 2 else 1
            kr2 = slice(64 * khalf, 64 * khalf + 64)
            kk0 = kt * 128 - khalf * (S // 2)
            sbk = P.bank()
            P.mm(sbk[:, q0:TT], self.KA[:, kt * 128:(kt + 1) * 128], QA[:, q0:TT], start=True, stop=False)
            P.mm(sbk[:, q0:TT], self.KB[kr2, kk0:kk0 + 128], QB[kr2, q0:TT], start=False, stop=True)
            pt = self.PT[kt % 3]
            P.act(pt[:, q0:TT], sbk[:, q0:TT], AF.Exp, bias=nb)
            if j >= 0:
                P.memset(pt[64:128, q0:q0 + 64], 0.0)
            for sub in range(max(j, 0), 4):
                P.mm(acc[sub][:, 0:132], pt[:, sub * 128:(sub + 1) * 128], self.V[:, kt, :],
                     start=(kt == 0), stop=(kt == 4 * i + sub))
            yield
        yout = W["yout"]
        for sub in range(4):
            rinv = self.sc("rinv", (128, 1))
            P.recip(rinv.v(), acc[sub][:, 128:129])
            osb = self.sc("osb")
            P.ts(osb.v(), acc[sub][:, 0:128], rinv.v(), ALU.mult)
            tb = P.bank()
            P.tr(tb[:, 0:128], osb.v(), ident)
            P.tt(yout[:, sub * 128:(sub + 1) * 128], tb[:, 0:128], W["gm"][:, sub * 128:(sub + 1) * 128], ALU.mult)
            yield
        P.dma(ST_ENG, self.ydst(0, tok0), yout.v())

    def ssd_gen(self, i):
        inproj = self.inproj
        P, W = self.P, self.W
        tok0 = i * TT
        ident = self.c("ident")
        ones = self.c("ones")
        tri = self.c("tri")
        b5 = inproj(5)
        P.act(W["sz"].v(), b5[:, :], AF.Silu)
        outs = [W["xsT"], W["BT"], W["CT"]]
        for j, (gi, nm) in enumerate(((6, "xs"), (7, "B"), (8, "C"))):
            bk = inproj(gi)
            cb = self.cbuf[j]
            P.act(cb[:, 3:3 + TT], bk[:, :], AF.Identity)
            o_w = PC["cw_" + nm][0]
            acc = W["t0"]
            P.ts(acc.v(), cb[:, 3:3 + TT], self.pcol[:, o_w + 3:o_w + 4], ALU.mult, self.pc("cb_" + nm), ALU.add)
            for tap in range(3):
                P.stt(acc.v(), cb[:, tap:tap + TT], self.pcol[:, o_w + tap:o_w + tap + 1], acc.v(), ALU.mult, ALU.add)
            P.act(outs[j].v(), acc.v(), AF.Silu)
            P.cp(cb[:, 0:3], cb[:, TT:TT + 3])
            yield
        P.cp(W["BTb"].v(), W["BT"].v(), eng="pool")
        P.cp(W["CTb"].v(), W["CT"].v(), eng="pool")
        dtb = [W["dtb0"], W["dtb1"]]
        ac = [W["ac0"], W["ac1"]]
        for h in range(2):
            bk = inproj(9 + h)
            xm = W["t1"]
            P.ts(xm.v(), bk[:, :], self.pc("dtb%d" % h), ALU.add, 40.0, ALU.min)
            P.act(xm.v(), xm.v(), AF.Exp)
            P.act(xm.v(), xm.v(), AF.Ln, bias=1.0)
            P.stt(dtb[h].v(), bk[:, :], self.pc("dtb%d" % h), xm.v(), ALU.add, ALU.max)
            dta = W["t2"]
            P.ts(dta.v(), dtb[h].v(), self.ah[:, h:h + 1], ALU.mult)
            P.scan(ac[h].v(), self.c("rm128"), dta.v(), 0.0, ALU.mult, ALU.add)
            yield
        xsT, BT, CT, BTb, CTb = W["xsT"], W["BT"], W["CT"], W["BTb"], W["CTb"]
        ytok = W["t3"]
        yT = W["t4"]
        for cch in range(4):
            cs = slice(cch * 128, cch * 128 + 128)
            last = cch * 128 + 127
            bcb = P.bank()
            P.mm(bcb[:, 0:128], BTb[:, cs], CTb[:, cs])
            cbm = self.sc("cbm")
            P.tt(cbm.v(), bcb[:, 0:128], tri, ALU.mult)
            xdtT = self.sc("xdtTb", (128, 128), BF16)
            dte = self.sc("dte")
            for h in range(2):
                hp = slice(64 * h, 64 * h + 64)
                P.tt(xdtT[hp, :], xsT[hp, cs], dtb[h][hp, cs], ALU.mult)
                P.act(dte[hp, :], ac[h][hp, cs], AF.Exp, scale=-1.0, bias=ac[h][hp, last:last + 1])
            yield
            xdteT = self.sc("xdteTb", (128, 128), BF16)
            P.tt(xdteT.v(), xdtT.v(), dte.v(), ALU.mult)
            btr = P.bank()
            btrb = View(btr.h[:].bitcast(BF16), btr.d)
            identb = self.cbf[:, 0, :]
            P.tr(btrb[:, 0:128], xdtT.v(), identb)
            P.tr(btrb[:, 128:256], xdteT.v(), identb)
            P.tr(btrb[:, 256:384], BTb[:, cs], identb)
            tok3 = self.sc("tok3", (128, 384), BF16)
            P.act(tok3.v(), btrb[:, 0:384], AF.Identity)
            xdt, xdte, Btok = tok3[:, 0:128], tok3[:, 128:256], tok3[:, 256:384]
            yield
            by = P.bank()
            for h in range(2):
                hc = slice(64 * h, 64 * h + 64)
                acol = self.sc("acol", (128, 1))
                junk = self.sc("junk")
                P.stt(junk.v(), ac[h][:, cs], 1.0, ident, ALU.mult, ALU.mult, accum=acol.v())
                seg = self.sc("seg")
                P.ts(seg.v(), ac[h][:, cs], acol.v(), ALU.subtract, 0.0, ALU.min)
                P.act(seg.v(), seg.v(), AF.Exp)
                MT = self.sc("MT", (128, 128), BF16)
                P.tt(MT.v(), cbm.v(), seg.v(), ALU.mult, eng="pool")
                eac = self.sc("eac")
                P.act(eac.v(), ac[h][:, cs], AF.Exp)
                CpT = self.sc("CpT", (128, 128), BF16)
                P.tt(CpT.v(), CT[:, cs], eac.v(), ALU.mult)
                P.mm(by[:, hc], MT.v(), xdt[:, hc], start=True, stop=False)
                P.mm(by[:, hc], CpT.v(), self.prev_bf[:, hc], start=False, stop=True)
            yield
            bst = P.bank()
            P.mm(bst[:, 0:128], Btok, xdte)
            for h in range(2):
                hc = slice(64 * h, 64 * h + 64)
                cd = self.sc("cd", (128, 1))
                P.act(cd.v(), ac[h][:, last:last + 1], AF.Exp)
                P.stt(self.prev[:, hc], self.prev[:, hc], cd.v(), bst[:, hc], ALU.mult, ALU.add)
            P.cp(self.prev_bf.v(), self.prev.v())
            yield
            ysb = self.sc("ysb")
            P.act(ysb.v(), by[:, 0:128], AF.Identity)
            byT = P.bank()
            P.tr(byT[:, 0:128], ysb.v(), ident)
            P.stt(yT[:, cs], xsT[:, cs], self.pc("dskip"), byT[:, 0:128], ALU.mult, ALU.add)
        yield
        P.tt(yT.v(), yT.v(), W["sz"].v(), ALU.mult)
        sq = W["t5"]
        P.act(sq.v(), yT.v(), AF.Square)
        bsq = P.bank()
        P.mm(bsq[0:1, :], ones[:, 0:1], sq.v())
        P.cp(W["ssqr"].v(), bsq[0:1, :])
        P.dma(ST_ENG, self.ssq_out[0:1, tok0:tok0 + TT], W["ssqr"].v())
        P.ts(W["yout2"].v(), yT.v(), self.pc("ssdnw"), ALU.mult)
        P.dma(ST_ENG, self.ydst(1, tok0), W["yout2"].v())

    def rwkv_gen(self, i):
        inproj = self.inproj
        P, W = self.P, self.W
        tok0 = i * TT
        ident = self.c("ident")
        ones = self.c("ones")
        bd = self.c("bd")
        mst, mit, ms = self.c("mst"), self.c("mit"), self.c("ms")
        outs = [W["r_s"], W["k_s"], W["v_s"], W["lo_s"]]
        for j, gi in enumerate((11, 12, 13, 14)):
            bk = inproj(gi)
            sbf = self.sbuf[j]
            P.act(sbf[:, 1:1 + TT], bk[:, :], AF.Identity)
            tmp = W["t0"]
            P.ts(tmp.v(), sbf[:, 1:1 + TT], self.ommu[:, j:j + 1], ALU.mult)
            o_mu = PC["mu_r"][0] + j
            P.stt(outs[j].v(), sbf[:, 0:TT], self.pcol[:, o_mu:o_mu + 1], tmp.v(), ALU.mult, ALU.add)
            P.cp(sbf[:, 0:1], sbf[:, TT:TT + 1])
        bg = inproj(15)
        P.act(W["sg"].v(), bg[:, :], AF.Silu)
        yield
        r_s, k_s, v_s, lo_s = outs
        th = W["t0"]
        P.act(th.v(), lo_s.v(), AF.Tanh)
        bw = P.bank()
        P.mm(bw[:, :], self.lora[:, 0:128], th.v())
        logw = W["logw"]
        P.act(logw.v(), bw[:, :], AF.Sigmoid, bias=self.pc("w0"))
        P.ts(logw.v(), logw.v(), -DECAY_SCALE, ALU.mult)
        ba = P.bank()
        P.mm(ba[:, :], self.lora[:, 128:256], lo_s.v())
        a_s = W["a_s"]
        P.act(a_s.v(), ba[:, :], AF.Sigmoid, bias=self.pc("a0"))
        kkn = W["kkn"]
        P.ts(kkn.v(), k_s.v(), self.pc("kk"), ALU.mult)
        sq = self.bfv("t1")
        P.act(sq, kkn.v(), AF.Square)
        bss = P.bank()
        P.mm(bss[:, :], self.cbf[:, 3, :], sq)
        rn = W["t2"]
        P.act(rn.v(), bss[:, :], AF.Sqrt)
        P.ts(rn.v(), rn.v(), 1e-12, ALU.max)
        P.recip(rn.v(), rn.v())
        P.tt(kkn.v(), kkn.v(), rn.v(), ALU.mult)
        kp = W["kp"]
        P.ts(kp.v(), a_s.v(), self.pc("ka"), ALU.mult, self.omka, ALU.add)
        P.tt(kp.v(), kp.v(), k_s.v(), ALU.mult)
        yield
        lw = W["lw"]
        P.scan(lw.v(), self.c("rm64"), logw.v(), 0.0, ALU.mult, ALU.add)
        ew = W["t3"]
        P.act(ew.v(), lw.v(), AF.Exp)
        ewn = W["t4"]
        P.act(ewn.v(), lw.v(), AF.Exp, scale=-1.0)
        lx = W["t5"]
        P.tt(lx.v(), lw.v(), logw.v(), ALU.subtract)
        P.act(lx.v(), lx.v(), AF.Exp)
        rt, kt, bt, at = W["rt"], W["kt"], W["bt"], W["at"]
        P.tt(rt.v(), r_s.v(), ew.v(), ALU.mult)
        P.tt(kt.v(), kp.v(), ewn.v(), ALU.mult)
        P.tt(bt.v(), kkn.v(), a_s.v(), ALU.mult)
        P.tt(bt.v(), bt.v(), ewn.v(), ALU.mult)
        P.stt(at.v(), kkn.v(), -1.0, lx.v(), ALU.mult, ALU.mult)
        rk = W["t1"]
        P.stt(rk.v(), r_s.v(), self.pc("rk"), kp.v(), ALU.mult, ALU.mult)
        yield
        yrw = W["yout3"]
        for pr in range(4):
            ps = slice(pr * 128, pr * 128 + 128)
            btr = P.bank()
            P.tr(btr[:, 0:128], v_s[:, ps], ident)
            P.tr(btr[:, 128:256], (v_s if KDBG else bt)[:, ps], ident)
            P.tr(btr[:, 256:384], (v_s if KDBG else kt)[:, ps], ident)
            P.tr(btr[:, 384:512], (v_s if KDBG else at)[:, ps], ident)
            tk = self.sc("rtok", (128, 512))
            for q_ in range(4):
                P.cp(tk[:, q_ * 128:(q_ + 1) * 128], btr[:, q_ * 128:(q_ + 1) * 128])
            Vtok, Btok, Ktok, Atok = tk[:, 0:128], tk[:, 128:256], tk[:, 256:384], tk[:, 384:512]
            Zc = self.sc("Zc", (128, 2, 128))
            P.cp(Zc[:, :, 0:64], Atok.re("p (h j) -> p h j", h=2))
            yield
            ArbT, ArkT, XTs, Xs, AakTs = [], [], [], [], []
            for h in range(2):
                hp = slice(64 * h, 64 * h + 64)
                bm = P.bank()
                P.mm(bm[:, 0:128], bt[hp, ps], at[hp, ps])
                P.mm(bm[:, 128:256], at[hp, ps], bt[hp, ps])
                P.mm(bm[:, 256:384], kt[hp, ps], at[hp, ps])
                XT = self.sc("XTb%d" % h, (128, 128), BF16)
                X = self.sc("Xb%d" % h, (128, 128), BF16)
                AakT = self.sc("AakT%d" % h)
                P.tt(XT.v(), bm[:, 0:128], mst, ALU.mult)
                P.tt(X.v(), bm[:, 128:256], ms, ALU.mult)
                P.tt(AakT.v(), bm[:, 256:384], mst, ALU.mult)
                XTs.append(XT)
                Xs.append(X)
                AakTs.append(AakT)
            for h in range(2):
                hp = slice(64 * h, 64 * h + 64)
                bm2 = P.bank()
                P.mm(bm2[:, 0:128], bt[hp, ps], rt[hp, ps])
                P.mm(bm2[:, 128:256], kt[hp, ps], rt[hp, ps])
                a1 = self.sc("ArbT%d" % h)
                a2 = self.sc("ArkT%d" % h)
                P.tt(a1.v(), bm2[:, 0:128], mit, ALU.mult)
                P.tt(a2.v(), bm2[:, 128:256], mit, ALU.mult)
                ArbT.append(a1)
                ArkT.append(a2)
            yield
            for h in range(2):
                bz = P.bank()
                P.mm(bz[:, 0:128], AakTs[h].v(), Vtok)
                P.cp(Zc[:, h, 64:128], bz[:, 64 * h:64 * h + 64])
            Zb = self.sc("Zb", (128, 2, 128), BF16)
            P.cp(Zb.v(), Zc.v())
            for k in range(6):
                yield
                for h in range(2):
                    bzz = P.bank()
                    Zh = Zc[:, h, :]
                    P.mm(bzz[:, 0:128], XTs[h].v(), Zb[:, h, :])
                    P.tt(Zh, Zh, bzz[:, 0:128], ALU.add)
                    if k < 5:
                        P.cp(Zb[:, h, :], Zh)
                if k < 5:
                    for h in range(2):
                        bsqr = P.bank()
                        P.mm(bsqr[:, 0:128], Xs[h].v(), XTs[h].v())
                        if k < 4:
                            P.mm(bsqr[:, 128:256], XTs[h].v(), Xs[h].v())
                        P.act(XTs[h].v(), bsqr[:, 0:128], AF.Identity)
                        if k < 4:
                            P.cp(Xs[h].v(), bsqr[:, 128:256])
            yield
            AV = self.sc("AV", (128, 2, 128))
            P.cp(AV[:, 0, :].re("p (h j) -> p h j", h=2), Zc[:, :, 0:64])
            P.cp(AV[:, 1, :].re("p (h j) -> p h j", h=2), Zc[:, :, 64:128])
            Ahat = AV[:, 0, :]
            Vhat = AV[:, 1, :]
            rh0 = self.rh0[pr % 2]
            rh1 = self.rh1[pr % 2]
            byl = P.bank()
            for h in range(2):
                hp = slice(64 * h, 64 * h + 64)
                br = P.bank()
                P.mm(br[:, 0:128], Ahat, ArbT[h].v())
                P.tt(rh0[hp, 0:64], rt[hp, pr * 128:pr * 128 + 64], br[hp, 0:64], ALU.add)
                P.tt(rh1[hp, 64:128], rt[hp, pr * 128 + 64:pr * 128 + 128], br[hp, 64:128], ALU.add)
                P.mm(byl[:, 128 * h:128 * h + 128], ArbT[h].v(), Vhat, start=True, stop=False)
                P.mm(byl[:, 128 * h:128 * h + 128], ArkT[h].v(), Vtok, start=False, stop=True)
            Yloc = self.sc("Yloc")
            for h in range(2):
                P.act(Yloc[:, 64 * h:64 * h + 64], byl[:, 128 * h + 64 * h:128 * h + 64 * h + 64], AF.Identity)
            yield
            PTc, Qc = [], []
            for c in range(2):
                scr = slice(64 * c, 64 * c + 64)
                wl = W["t3"][:, pr * 128 + 64 * c + 63: pr * 128 + 64 * c + 64]
                bp_ = P.bank()
                P.mm(bp_[:, 0:128], Ahat[scr, :], Btok[scr, :])
                P.mm(bp_[:, 128:256], Btok[scr, :], Vhat[scr, :], start=True, stop=False)
                P.mm(bp_[:, 128:256], Ktok[scr, :], Vtok[scr, :], start=False, stop=True)
                pt_ = self.sc("PTc%d" % c)
                P.stt(pt_.v(), bp_[:, 0:128], 1.0, bd, ALU.mult, ALU.mult)
                P.tt(pt_.v(), pt_.v(), ident, ALU.add)
                qc_ = self.sc("Qc%d" % c)
                P.stt(qc_.v(), bp_[:, 128:256], wl, bd, ALU.mult, ALU.mult)
                PTc.append(pt_)
                Qc.append(qc_)
            yield
            bY = P.bank()
            for c in range(2):
                Scur = self.Sbd[self.sbd_i % 2]
                Snew = self.Sbd[(self.sbd_i + 1) % 2]
                wl = W["t3"][:, pr * 128 + 64 * c + 63: pr * 128 + 64 * c + 64]
                P.mm(bY[:, 0:128], (rh0 if c == 0 else rh1).v(), Scur.v(), start=(c == 0), stop=(c == 1))
                bS = P.bank()
                P.mm(bS[:, 0:128], PTc[c].v(), Scur.v())
                P.stt(Snew.v(), bS[:, 0:128], wl, Qc[c].v(), ALU.mult, ALU.add)
                self.sbd_i += 1
            y = self.sc("rwy")
            P.tt(y.v(), bY[:, 0:128], Yloc.v(), ALU.add)
            yield
            st = self.sc("gnst", (128, 16))
            y3 = y.v().re("p (h i) -> p h i", h=2)
            P.reduce(st[:, 0:2], y3, ALU.add)
            ysq = self.sc("ysq")
            P.tt(ysq.v(), y.v(), y.v(), ALU.mult)
            P.reduce(st[:, 2:4], ysq.v().re("p (h i) -> p h i", h=2), ALU.add)
            P.ts(st[:, 4:6], st[:, 0:2], 1.0 / 64, ALU.mult)
            P.tt(st[:, 6:8], st[:, 4:6], st[:, 4:6], ALU.mult)
            P.stt(st[:, 8:10], st[:, 2:4], 1.0 / 64, st[:, 6:8], ALU.mult, ALU.subtract)
            P.act(st[:, 10:12], st[:, 8:10], AF.Sqrt, bias=GN_EPS)
            P.recip(st[:, 12:14], st[:, 10:12])
            bb = P.bank()
            P.mm(bb[:, 0:128], rk[:, ps], bd)
            bon = self.sc("bon")
            P.tt(bon.v(), bb[:, 0:128], Vtok, ALU.mult)
            yn = self.sc("yn")
            for h in range(2):
                hc = slice(64 * h, 64 * h + 64)
                P.ts(yn[:, hc], y[:, hc], st[:, 4 + h:5 + h], ALU.subtract, st[:, 12 + h:13 + h], ALU.mult)
            P.tt(yn.v(), yn.v(), self.lnrow[:, 0:128], ALU.mult)
            P.tt(yn.v(), yn.v(), self.lnrow[:, 128:256], ALU.add)
            P.tt(yn.v(), yn.v(), bon.v(), ALU.add)
            bt_ = P.bank()
            P.tr(bt_[:, 0:128], yn.v(), ident)
            P.tt(yrw[:, ps], bt_[:, 0:128], W["sg"][:, ps], ALU.mult)
        P.dma(ST_ENG, self.ydst(2, tok0), yrw.v())
        if isinstance(self.y_loc, list) and (i % 2 == 1):
            ck = tok0 // 1024
            P.allgather(self.y_all[ck].v(), self.y_loc[ck].v(), GROUPS)


def build_A(S):
    P = Prog()
    import os
    if int(os.environ.get("KPAD", "0")):
        P.sb([128, int(os.environ["KPAD"]) * 256], F32, "padtile")
    P.make_banks(8)
    P.gen_banks = 4
    io = {}
    x = P.dram("x", [S, D], F32, kind="ExternalInput")
    io["wcat"] = P.dram("wcat", [D, NG * 128], F32, kind="ExternalInput")
    io["adaw"] = P.dram("adaw", [D, 2048], F32, kind="ExternalInput")
    io["pcol"] = P.dram("pcol", [128, NPC], F32, kind="ExternalInput")
    io["wuq"] = P.dram("wuq", [256, 384], F32, kind="ExternalInput")
    io["wukv"] = P.dram("wukv", [128, 256], F32, kind="ExternalInput")
    io["lora"] = P.dram("lora", [128, 256], F32, kind="ExternalInput")
    io["lnrow"] = P.dram("lnrow", [128, 256], F32, kind="ExternalInput")
    io["pos"] = P.dram("pos", [1, S], I32, kind="ExternalInput")
    io["cst"] = P.dram("cst", [128, NCST], F32, kind="ExternalInput")
    y_loc = P.dram("y_loc", [384, S], BF16, kind="ExternalOutput")
    ssq = P.dram("ssq", [1, S], F32, kind="ExternalOutput")
    PhaseA(P, S, x, y_loc, ssq, io).run()
    return P


def prep_B(inp, l, b, q, y_locs, ssqs, x_b, SB):
    f32 = np.float32
    t0, t1 = q * SB, (q + 1) * SB
    yT = np.stack([y_locs[g][br * 128:(br + 1) * 128, t0:t1] for g in range(4) for br in range(3)], axis=0)
    w_out = inp["w_out"][l]
    wout = np.stack([w_out[512 * br + 128 * g: 512 * br + 128 * g + 128, :] for g in range(4) for br in range(3)], axis=0)
    ssq = np.ascontiguousarray(np.concatenate([s[:, t0:t1] for s in ssqs], axis=0).T)
    return {
        "yT": np.ascontiguousarray(yT),
        "wout": np.ascontiguousarray(wout).astype(f32),
        "ssq4": ssq.astype(f32),
        "xin": np.ascontiguousarray(x_b[t0:t1]).astype(f32),
        "adawg": np.ascontiguousarray(inp["ada_w"][l][:, 2048:3072]).astype(f32),
        "cvec": _col8(inp["c"][b]),
        "gateb": np.ascontiguousarray(np.broadcast_to(inp["ada_b"][l][2048:3072], (128, 1024))).astype(f32),
        "fnw": np.ascontiguousarray(np.broadcast_to(inp["final_norm_w"], (128, 1024))).astype(f32),
        "onesb": np.ones((128, 128), f32),
    }


class PhaseB:
    def __init__(self, P, SB, io, xin, xo, xf, tag="B"):
        self.P, self.SB, self.io, self.xin, self.xo, self.xf, self.tag = P, SB, io, xin, xo, xf, tag

    def run(self):
        P, io, SB, tag = self.P, self.io, self.SB, self.tag
        sb = P.sb
        NT = SB // 128
        ones = sb([128, 128], F32, "onesB" + tag)
        P.dma("sp", ones.v(), io["onesb"].v())
        stg = [sb([128, 1024], F32, f"stgB{i}" + tag) for i in range(2)]
        w_sb = sb([128, 12, 1024], BF16, "woutsb" + tag)
        for c in range(12):
            s_ = stg[c % 2]
            P.dma("sp", s_.v(), io["wout"][c, :, :])
            P.cp(w_sb[:, c, :], s_.v(), eng="pool")
        cv = sb([128, 8], F32, "cvecB" + tag)
        P.dma("sp", cv.v(), io["cvec"].v())
        cact = sb([128, 8], F32, "cactB" + tag)
        P.act(cact.v(), cv.v(), AF.Silu)
        cbc = sb([128, 8, 128], F32, "cbcB" + tag)
        for kc in range(8):
            P.ts(cbc[:, kc, :], ones.v(), cact[:, kc:kc + 1], ALU.mult)
        gate = sb([128, 1024], F32, "gateB" + tag)
        P.dma("sp", gate.v(), io["gateb"].v())
        fnw = sb([128, 1024], F32, "fnwB" + tag)
        P.dma("sp", fnw.v(), io["fnw"].v())
        awv = io["adawg"].v().re("(c p) n -> p c n", p=128)
        for half in range(2):
            bk = P.bank()
            for kc in range(8):
                s_ = stg[kc % 2]
                P.dma("sp", s_[:, 0:512], awv[:, kc, half * 512:(half + 1) * 512])
                P.mm(bk[:, :], cbc[:, kc, :], s_[:, 0:512], start=(kc == 0), stop=(kc == 7))
            P.tt(gate[:, half * 512:(half + 1) * 512], gate[:, half * 512:(half + 1) * 512], bk[:, :], ALU.add)
        sq4 = sb([128, NT, 4], F32, "sq4B" + tag)
        P.dma("sp", sq4.v(), io["ssq4"].v().re("(n p) g -> p n g", p=128))
        rstd = sb([128, NT], F32, "rstdB" + tag)
        P.reduce(rstd.v(), sq4.v(), ALU.add)
        P.act(rstd.v(), rstd.v(), AF.Sqrt, scale=1.0 / 512, bias=EPS)
        P.recip(rstd.v(), rstd.v())
        yt = [sb([128, 12, 128], BF16, f"ytB{i}" + tag) for i in range(2)]
        xt = [sb([128, 1024], F32, f"xtB{i}" + tag) for i in range(2)]
        xn = [sb([128, 1024], F32, f"xnB{i}" + tag) for i in range(2)]
        t1 = sb([128, 512], F32, "t1B" + tag)
        xfo = sb([128, 1024], F32, "xfB" + tag)
        st = sb([128, 4], F32, "stB" + tag)
        yv = io["yT"].v()
        for t in range(NT):
            y_ = yt[t % 2]
            x_ = xt[t % 2]
            xn_ = xn[t % 2]
            P.dma("sp", y_.v(), yv[:, :, t * 128:(t + 1) * 128].re("c p n -> p c n"))
            P.dma("sp", x_.v(), self.xin[t * 128:(t + 1) * 128, :])
            for half in range(2):
                hs = slice(half * 512, (half + 1) * 512)
                bo = P.bank()
                bs = P.bank()
                no = ns = 0
                for c in range(12):
                    br = c % 3
                    if br == 1:
                        P.mm(bs[:, :], y_[:, c, :], w_sb[:, c, hs], start=(ns == 0), stop=(ns == 3))
                        ns += 1
                    else:
                        P.mm(bo[:, :], y_[:, c, :], w_sb[:, c, hs], start=(no == 0), stop=(no == 7))
                        no += 1
                P.act(t1.v(), bs[:, :], AF.Identity, scale=rstd[:, t:t + 1])
                P.tt(t1.v(), t1.v(), bo[:, :], ALU.add)
                P.tt(t1.v(), t1.v(), gate[:, hs], ALU.mult)
                P.tt(xn_[:, hs], t1.v(), x_[:, hs], ALU.add, eng="pool")
            P.dma("sp", self.xo[t * 128:(t + 1) * 128, :], xn_.v())
            P.act(xfo.v(), xn_.v(), AF.Square, accum=st[:, 0:1])
            P.act(st[:, 1:2], st[:, 0:1], AF.Sqrt, scale=1.0 / D, bias=EPS)
            P.recip(st[:, 2:3], st[:, 1:2])
            P.stt(xfo.v(), xn_.v(), st[:, 2:3], fnw.v(), ALU.mult, ALU.mult)
            P.dma("sp", self.xf[t * 128:(t + 1) * 128, :], xfo.v())


def build_B(SB):
    P = Prog()
    P.make_banks(8)
    io = {}
    io["yT"] = P.dram("yT", [12, 128, SB], BF16, kind="ExternalInput")
    io["wout"] = P.dram("wout", [12, 128, 1024], F32, kind="ExternalInput")
    io["ssq4"] = P.dram("ssq4", [SB, 4], F32, kind="ExternalInput")
    xin = P.dram("xin", [SB, D], F32, kind="ExternalInput")
    io["adawg"] = P.dram("adawg", [D, 1024], F32, kind="ExternalInput")
    io["cvec"] = P.dram("cvec", [128, 8], F32, kind="ExternalInput")
    io["gateb"] = P.dram("gateb", [128, 1024], F32, kind="ExternalInput")
    io["fnw"] = P.dram("fnw", [128, 1024], F32, kind="ExternalInput")
    io["onesb"] = P.dram("onesb", [128, 128], F32, kind="ExternalInput")
    xo = P.dram("xo", [SB, D], F32, kind="ExternalOutput")
    xf = P.dram("xf", [SB, D], F32, kind="ExternalOutput")
    PhaseB(P, SB, io, xin, xo, xf).run()
    return P


_CACHE = {}


def _prog(kind, n):
    key = (kind, n)
    if key not in _CACHE:
        P = build_A(n) if kind == "A" else build_B(n)
        _CACHE[key] = P.finish()
    return _CACHE[key]


def kernel_unfused(**inputs):
    inp = {k: np.asarray(v) for k, v in inputs.items()}
    B, S, _ = inp["x"].shape
    SB = S // 4
    ncA = _prog("A", S)
    ncB = _prog("B", SB)
    x_cur = [np.ascontiguousarray(inp["x"][b]).astype(np.float32) for b in range(B)]
    out = None
    for l in range(2):
        insA = []
        for b in range(B):
            for g in range(4):
                d = prep_A(inp, l, b, g, S)
                d["x"] = x_cur[b]
                insA.append(d)
        resA = run_bass_kernel_spmd(ncA, insA, core_ids=list(range(8))).results
        insB = []
        for b in range(B):
            ys = [np.asarray(resA[b * 4 + g]["y_loc"]) for g in range(4)]
            sq = [np.asarray(resA[b * 4 + g]["ssq"]) for g in range(4)]
            for q in range(4):
                insB.append(prep_B(inp, l, b, q, ys, sq, x_cur[b], SB))
        resB = run_bass_kernel_spmd(ncB, insB, core_ids=list(range(8))).results
        x_cur = [np.concatenate([np.asarray(resB[b * 4 + q]["xo"]) for q in range(4)], axis=0) for b in range(B)]
        if l == 1:
            out = np.stack([np.concatenate([np.asarray(resB[b * 4 + q]["xf"]) for q in range(4)], axis=0)
                            for b in range(B)], axis=0)
    return out.astype(np.float32)


GROUPS = [[0, 1, 2, 3], [4, 5, 6, 7]]


def prep_F(inp, b, g, S):
    f32 = np.float32
    d = {"x": np.ascontiguousarray(inp["x"][b]).astype(f32)}
    for l in range(2):
        a = prep_A(inp, l, b, g, S)
        for k in ("wcat", "adaw", "pcol", "wuq", "wukv", "lora", "lnrow"):
            d[f"{k}{l}"] = a[k]
        if l == 0:
            d["pos"] = a["pos"]
            d["cst"] = a["cst"]
        w_out = inp["w_out"][l]
        d[f"wout{l}"] = np.ascontiguousarray(
            np.stack([w_out[512 * br + 128 * g: 512 * br + 128 * g + 128, :] for br in range(3)], axis=0)).astype(f32)
        d[f"adawg{l}"] = np.ascontiguousarray(inp["ada_w"][l][:, 2048:3072]).astype(f32)
        d[f"gateb{l}"] = np.ascontiguousarray(np.broadcast_to(inp["ada_b"][l][2048:3072], (128, 1024))).astype(f32)
    d["fnw"] = np.ascontiguousarray(np.broadcast_to(inp["final_norm_w"], (128, 1024))).astype(f32)
    return d


class PhaseBC:
    def __init__(self, P, A, S):
        self.P, self.A, self.S = P, A, S
        self.rstd = P.sb([128, S // 128], F32, "rstdBC")
        self.sq64 = P.sb([S // 128, 128], F32, "sq64BC")
        self.st = P.sb([128, 4], F32, "stBC")

    def run(self, l, last, io, x_src, y_loc, ssq_loc, ssq_sum, part_loc, part_sum, xcur, out):
        P, A, S = self.P, self.A, self.S
        W, G = A.W, A.G
        NT = S // 128
        ident = A.c("ident")
        ones = A.c("ones")
        P.allreduce(ssq_sum.v(), ssq_loc.v(), GROUPS)
        P.dma("sp", self.sq64.v(), ssq_sum.v().re("o (n p) -> (o n) p", p=128))
        bk = P.bank()
        P.tr(bk[:, 0:NT], self.sq64.v(), ident[0:NT, 0:NT])
        P.act(self.rstd.v(), bk[:, 0:NT], AF.Sqrt, scale=1.0 / 512, bias=EPS)
        P.recip(self.rstd.v(), self.rstd.v())
        wo = [View(G[c].h[:].bitcast(BF16), G[c].d) for c in range(3)]
        for c in range(3):
            s_ = A.xt[c % 2]
            P.dma("sp", s_.v(), io["wout"][c, :, :])
            P.cp(wo[c], s_.v(), eng="pool")
        ytl = [View(W["t0"].h[:].bitcast(BF16), W["t0"].d), View(W["t1"].h[:].bitcast(BF16), W["t1"].d)]
        hT32 = View(W["hT"].h[:].rearrange("p a b -> p (a b)").bitcast(F32), W["hT"].d)
        hTa = View(hT32.ap[:, 0:1024], Dep("hTa"))
        hTb = View(hT32.ap[:, 1024:2048], Dep("hTb"))
        for v_ in (hTa, hTb):
            v_.dep.lw = W["hT"].d.lw
            v_.dep.rd = dict(W["hT"].d.rd)
        parts = [A.xn[0].v(), hTa]
        tmp = W["t2"]
        yv = y_loc.v().re("(c p) s -> p c s", p=128)
        for t in range(NT):
            yt = ytl[t % 2]
            ytv = yt[:, 0:384].re("p (c s) -> p c s", c=3)
            P.dma("sp", ytv, yv[:, :, t * 128:(t + 1) * 128])
            for half in range(2):
                hs = slice(half * 512, (half + 1) * 512)
                bo = P.bank()
                bs = P.bank()
                P.mm(bo[:, :], ytv[:, 0, :], wo[0][:, hs], start=True, stop=False)
                P.mm(bo[:, :], ytv[:, 2, :], wo[2][:, hs], start=False, stop=True)
                P.mm(bs[:, :], ytv[:, 1, :], wo[1][:, hs], start=True, stop=True)
                part = parts[t % 2]
                P.act(tmp.v(), bs[:, :], AF.Identity, scale=self.rstd[:, t:t + 1])
                P.tt(part[:, hs], tmp.v(), bo[:, :], ALU.add)
            ck, r0 = t // 8, (t % 8) * 128
            P.dma("sp", part_loc[ck][r0:r0 + 128, :], parts[t % 2], semdep=parts[t % 2].dep)
            if t % 8 == 7:
                P.allreduce(part_sum[ck].v(), part_loc[ck].v(), GROUPS)
        cbc = hTa.re("p (k m) -> p k m", k=8)
        gate = hTb
        for kc in range(8):
            P.ts(cbc[:, kc, :], ones, A.cact[:, kc:kc + 1], ALU.mult)
        P.dma("sp", gate, io["gateb"].v())
        awv = io["adawg"].v().re("(c p) n -> p c n", p=128)
        stg = [W["t3"], W["t4"]]
        for half in range(2):
            bk = P.bank()
            for kc in range(8):
                s_ = stg[kc % 2]
                P.dma("sp", s_.v(), awv[:, kc, half * 512:(half + 1) * 512])
                P.mm(bk[:, :], cbc[:, kc, :], s_.v(), start=(kc == 0), stop=(kc == 7))
            P.tt(gate[:, half * 512:(half + 1) * 512], gate[:, half * 512:(half + 1) * 512], bk[:, :], ALU.add)
        fnw = A.xn[0]
        if last:
            P.dma("sp", fnw.v(), io["fnw"].v())
        XH = [G[k] for k in range(8)]
        PH = [G[8 + k] for k in range(7)] + [W["t5"]]
        st = self.st
        n = 0
        for t in range(NT):
            rows = slice(t * 128, (t + 1) * 128)
            pr0 = (t % 8) * 128
            xs_, ps_ = [], []
            for half in range(2):
                hs = slice(half * 512, (half + 1) * 512)
                xh = XH[n % 8]
                ph = PH[n % 8]
                n += 1
                P.dma("sp", xh.v(), x_src[rows, hs])
                P.dma("sp", ph.v(), part_sum[t // 8][pr0:pr0 + 128, hs])
                P.tt(ph.v(), ph.v(), gate[:, hs], ALU.mult)
                P.tt(xh.v(), xh.v(), ph.v(), ALU.add)
                if not last:
                    P.dma("sp", xcur[rows, hs], xh.v(), semdep=xh.d)
                else:
                    P.act(ph.v(), xh.v(), AF.Square, accum=st[:, half:half + 1])
                xs_.append(xh)
                ps_.append(ph)
            if last:
                P.tt(st[:, 2:3], st[:, 0:1], st[:, 1:2], ALU.add)
                P.act(st[:, 3:4], st[:, 2:3], AF.Sqrt, scale=1.0 / D, bias=EPS)
                P.recip(st[:, 3:4], st[:, 3:4])
                for half in range(2):
                    hs = slice(half * 512, (half + 1) * 512)
                    P.stt(ps_[half].v(), xs_[half].v(), st[:, 3:4], fnw[:, hs], ALU.mult, ALU.mult)
                    P.dma("sp", out[rows, hs], ps_[half].v(), semdep=ps_[half].d)
        for v_ in (hTa, hTb):
            for k_, val in list(v_.dep.rd.items()) + ([v_.dep.lw] if v_.dep.lw else []):
                if W["hT"].d.rd.get(k_, 0) < val:
                    W["hT"].d.rd[k_] = val


def build_fused(S):
    P = Prog()
    P.make_banks(8)
    P.gen_banks = 4
    x0 = P.dram("x", [S, D], F32, kind="ExternalInput")
    out = P.dram("out", [S, D], F32, kind="ExternalOutput")
    xcur = P.dram("xcur", [S, D], F32)
    shared = {"pos": P.dram("pos", [1, S], I32, kind="ExternalInput"),
              "cst": P.dram("cst", [128, NCST], F32, kind="ExternalInput")}
    fnw = P.dram("fnw", [128, 1024], F32, kind="ExternalInput")
    A = None
    BC = None
    for l in range(2):
        io = dict(shared)
        io["wcat"] = P.dram(f"wcat{l}", [D, NG * 128], F32, kind="ExternalInput")
        io["adaw"] = P.dram(f"adaw{l}", [D, 2048], F32, kind="ExternalInput")
        io["pcol"] = P.dram(f"pcol{l}", [128, NPC], F32, kind="ExternalInput")
        io["wuq"] = P.dram(f"wuq{l}", [256, 384], F32, kind="ExternalInput")
        io["wukv"] = P.dram(f"wukv{l}", [128, 256], F32, kind="ExternalInput")
        io["lora"] = P.dram(f"lora{l}", [128, 256], F32, kind="ExternalInput")
        io["lnrow"] = P.dram(f"lnrow{l}", [128, 256], F32, kind="ExternalInput")
        io["wout"] = P.dram(f"wout{l}", [3, 128, 1024], F32, kind="ExternalInput")
        io["adawg"] = P.dram(f"adawg{l}", [D, 1024], F32, kind="ExternalInput")
        io["gateb"] = P.dram(f"gateb{l}", [128, 1024], F32, kind="ExternalInput")
        io["fnw"] = fnw
        y_loc = P.dram(f"y_loc{l}", [384, S], BF16)
        ssq_loc = P.dram(f"ssq_loc{l}", [1, S], F32)
        ssq_sum = P.dram(f"ssq_sum{l}", [1, S], F32)
        part_loc = [P.dram(f"part_loc{l}_{k}", [1024, D], F32) for k in range(S // 1024)]
        part_sum = [P.dram(f"part_sum{l}_{k}", [1024, D], F32) for k in range(S // 1024)]
        x_src = x0 if l == 0 else xcur
        if A is None:
            A = PhaseA(P, S, x_src, y_loc, ssq_loc, io)
        else:
            A.x, A.y_loc, A.ssq_out, A.io = x_src, y_loc, ssq_loc, io
        A.run()
        if BC is None:
            BC = PhaseBC(P, A, S)
        BC.run(l, l == 1, io, x_src, y_loc, ssq_loc, ssq_sum, part_loc, part_sum, xcur, out)
    return P


def kernel(**inputs):
    inp = {k: np.asarray(v) for k, v in inputs.items()}
    B, S, _ = inp["x"].shape
    key = ("F2", S)
    if key not in _CACHE:
        _CACHE[key] = build_fused2(S).finish()
    nc = _CACHE[key]
    ins = [prep_F2(inp, b, g, S) for b in range(B) for g in range(4)]
    res = run_bass_kernel_spmd(nc, ins, core_ids=list(range(8))).results
    return np.stack([np.asarray(res[4 * b]["out"]) for b in range(B)], axis=0).astype(np.float32)


class PhaseD:
    def __init__(self, P, A, S):
        self.P, self.A, self.S = P, A, S
        NT = S // 128
        self.rstd = P.sb([128, NT], F32, "rstdD")
        self.sq64 = P.sb([NT, 4, 128], F32, "sq64D")
        self.st = P.sb([128, 4], F32, "stD")

    def run(self, l, last, io, x_src, y_all, ssq_all, xcur, out):
        P, A, S = self.P, self.A, self.S
        W, G = A.W, A.G
        NT = S // 128
        ident = A.c("ident")
        ones = A.c("ones")
        sq = self.sq64
        P.dma("sp", sq.v(), ssq_all.v().re("g (n p) -> n g p", p=128))
        P.tt(sq[:, 0, :], sq[:, 0, :], sq[:, 1, :], ALU.add)
        P.tt(sq[:, 2, :], sq[:, 2, :], sq[:, 3, :], ALU.add)
        P.tt(sq[:, 0, :], sq[:, 0, :], sq[:, 2, :], ALU.add)
        bk = P.bank()
        P.tr(bk[:, 0:NT], sq[:, 0, :], ident[0:NT, 0:NT])
        P.act(self.rstd.v(), bk[:, 0:NT], AF.Sqrt, scale=1.0 / 512, bias=EPS)
        P.recip(self.rstd.v(), self.rstd.v())
        wo = [View(G[c].h[:].bitcast(BF16), G[c].d) for c in range(12)]
        for c in range(12):
            s_ = A.xt[c % 2]
            P.dma("sp", s_.v(), io["woutf"][c, :, :])
            P.cp(wo[c], s_.v(), eng=("dve", "act")[c % 2])
        hT32 = View(W["hT"].h[:].rearrange("p a b -> p (a b)").bitcast(F32), W["hT"].d)
        hTa = View(hT32.ap[:, 0:1024], Dep("hTa"))
        hTb = View(hT32.ap[:, 1024:2048], Dep("hTb"))
        for v_ in (hTa, hTb):
            v_.dep.lw = W["hT"].d.lw
            v_.dep.rd = dict(W["hT"].d.rd)
        cbc = hTa.re("p (k m) -> p k m", k=8)
        gate = hTb
        for kc in range(8):
            P.ts(cbc[:, kc, :], ones, A.cact[:, kc:kc + 1], ALU.mult)
        P.dma("sp", gate, io["gateb"].v())
        awv = io["adawg"].v().re("(c p) n -> p c n", p=128)
        stg = [W["t4"], W["t5"]]
        for half in range(2):
            bk = P.bank()
            for kc in range(8):
                s_ = stg[kc % 2]
                P.dma("sp", s_.v(), awv[:, kc, half * 512:(half + 1) * 512])
                P.mm(bk[:, :], cbc[:, kc, :], s_.v(), start=(kc == 0), stop=(kc == 7))
            P.tt(gate[:, half * 512:(half + 1) * 512], gate[:, half * 512:(half + 1) * 512], bk[:, :], ALU.add)
        fnw = hTa
        if last:
            P.dma("sp", fnw, io["fnw"].v())
        ybuf = [(View(W["t0"].h[:].bitcast(BF16), W["t0"].d), View(W["t1"].h[:].bitcast(BF16), W["t1"].d)),
                (View(W["t2"].h[:].bitcast(BF16), W["t2"].d), View(W["t3"].h[:].bitcast(BF16), W["t3"].d))]
        OH = [G[12], G[13], G[14], W["t4"], W["t5"]]
        tmp = W["ssqr"]
        st = self.st
        n = 0
        for t in range(NT):
            rows = slice(t * 128, (t + 1) * 128)
            ck, c0 = t // 8, (t % 8) * 128
            ya, yb = ybuf[t % 2]
            ysrc = y_all[ck].v().re("(c p) s -> p c s", p=128)
            P.dma("sp", ya.re("p (c s) -> p c s", c=8), ysrc[:, 0:8, c0:c0 + 128])
            P.dma("sp", yb[:, 0:512].re("p (c s) -> p c s", c=4), ysrc[:, 8:12, c0:c0 + 128])
            x_ = A.xt[t % 2]
            P.dma("sp", x_.v(), x_src[rows, :])

            def ych(c):
                return ya[:, c * 128:(c + 1) * 128] if c < 8 else yb[:, (c - 8) * 128:(c - 7) * 128]

            ohs = []
            for half in range(2):
                hs = slice(half * 512, (half + 1) * 512)
                bo = P.bank()
                bs = P.bank()
                no = ns = 0
                for c in range(12):
                    if c % 3 == 1:
                        P.mm(bs[:, :], ych(c), wo[c][:, hs], start=(ns == 0), stop=(ns == 3))
                        ns += 1
                    else:
                        P.mm(bo[:, :], ych(c), wo[c][:, hs], start=(no == 0), stop=(no == 7))
                        no += 1
                oh = OH[n % 5]
                n += 1
                P.act(oh.v(), bs[:, :], AF.Identity, scale=self.rstd[:, t:t + 1])
                P.tt(oh.v(), oh.v(), bo[:, :], ALU.add)
                P.tt(oh.v(), oh.v(), gate[:, hs], ALU.mult)
                P.tt(oh.v(), oh.v(), x_[:, hs], ALU.add)
                if not last:
                    P.dma(ST_ENG, xcur[rows, hs], oh.v(), semdep=oh.d)
                else:
                    P.act(A.xn[0][:, hs], oh.v(), AF.Square, accum=st[:, half:half + 1])
                ohs.append(oh)
            if last:
                P.tt(st[:, 2:3], st[:, 0:1], st[:, 1:2], ALU.add)
                P.act(st[:, 3:4], st[:, 2:3], AF.Sqrt, scale=1.0 / D, bias=EPS)
                P.recip(st[:, 3:4], st[:, 3:4])
                for half in range(2):
                    hs = slice(half * 512, (half + 1) * 512)
                    P.stt(ohs[half].v(), ohs[half].v(), st[:, 3:4], fnw[:, hs], ALU.mult, ALU.mult)
                    P.dma(ST_ENG, out[rows, hs], ohs[half].v(), semdep=ohs[half].d)
        for v_ in (hTa, hTb):
            for k_, val in list(v_.dep.rd.items()) + ([v_.dep.lw] if v_.dep.lw else []):
                if W["hT"].d.rd.get(k_, 0) < val:
                    W["hT"].d.rd[k_] = val


def prep_F2(inp, b, g, S):
    d = prep_F(inp, b, g, S)
    for l in range(2):
        w_out = inp["w_out"][l]
        d[f"woutf{l}"] = np.ascontiguousarray(
            np.stack([w_out[512 * br + 128 * gg: 512 * br + 128 * gg + 128, :] for gg in range(4) for br in range(3)],
                     axis=0)).astype(np.float32)
        del d[f"wout{l}"]
    return d


def build_fused2(S):
    P = Prog()
    P.make_banks(8)
    P.gen_banks = 4
    x0 = P.dram("x", [S, D], F32, kind="ExternalInput")
    out = P.dram("out", [S, D], F32, kind="ExternalOutput")
    xcur = P.dram("xcur", [S, D], F32)
    shared = {"pos": P.dram("pos", [1, S], I32, kind="ExternalInput"),
              "cst": P.dram("cst", [128, NCST], F32, kind="ExternalInput")}
    fnw = P.dram("fnw", [128, 1024], F32, kind="ExternalInput")
    A = None
    Dp = None
    NCK = S // 1024
    for l in range(2):
        io = dict(shared)
        io["wcat"] = P.dram(f"wcat{l}", [D, NG * 128], F32, kind="ExternalInput")
        io["adaw"] = P.dram(f"adaw{l}", [D, 2048], F32, kind="ExternalInput")
        io["pcol"] = P.dram(f"pcol{l}", [128, NPC], F32, kind="ExternalInput")
        io["wuq"] = P.dram(f"wuq{l}", [256, 384], F32, kind="ExternalInput")
        io["wukv"] = P.dram(f"wukv{l}", [128, 256], F32, kind="ExternalInput")
        io["lora"] = P.dram(f"lora{l}", [128, 256], F32, kind="ExternalInput")
        io["lnrow"] = P.dram(f"lnrow{l}", [128, 256], F32, kind="ExternalInput")
        io["woutf"] = P.dram(f"woutf{l}", [12, 128, 1024], F32, kind="ExternalInput")
        io["adawg"] = P.dram(f"adawg{l}", [D, 1024], F32, kind="ExternalInput")
        io["gateb"] = P.dram(f"gateb{l}", [128, 1024], F32, kind="ExternalInput")
        io["fnw"] = fnw
        y_loc = [P.dram(f"y_loc{l}_{k}", [384, 1024], BF16) for k in range(NCK)]
        y_all = [P.dram(f"y_all{l}_{k}", [4 * 384, 1024], BF16) for k in range(NCK)]
        ssq_loc = P.dram(f"ssq_loc{l}", [1, S], F32)
        ssq_all = P.dram(f"ssq_all{l}", [4, S], F32)
        x_src = x0 if l == 0 else xcur
        if A is None:
            A = PhaseA(P, S, x_src, y_loc, ssq_loc, io)
        else:
            A.x, A.y_loc, A.ssq_out, A.io = x_src, y_loc, ssq_loc, io
        A.y_all = y_all
        A.run()
        P.allgather(ssq_all.v(), ssq_loc.v(), GROUPS)
        if Dp is None:
            Dp = PhaseD(P, A, S)
        Dp.run(l, l == 1, io, x_src, y_all, ssq_all, xcur, out)
    return P
```

```python
import numpy as np
import concourse.bass as bass
import concourse.mybir as mybir
from concourse.bass_utils import run_bass_kernel_spmd

F32 = mybir.dt.float32
BF16 = mybir.dt.bfloat16
I32 = mybir.dt.int32
ALU = mybir.AluOpType
AF = mybir.ActivationFunctionType
AX = mybir.AxisListType


class Dep:
    __slots__ = ("name", "lw", "rd", "sem", "cnt", "psum", "pend", "sem2", "cnt2")

    def __init__(self, name=""):
        self.name = name
        self.psum = False
        self.pend = None
        self.lw = None
        self.rd = {}
        self.sem = None
        self.cnt = 0
        self.sem2 = None
        self.cnt2 = 0


class View:
    __slots__ = ("ap", "dep")

    def __init__(self, ap, dep):
        self.ap = ap
        self.dep = dep

    def __getitem__(self, idx):
        return View(self.ap[idx], self.dep)

    def re(self, pat, **kw):
        return View(self.ap.rearrange(pat, **kw), self.dep)

    def bc(self, shape):
        return View(self.ap.broadcast_to(shape), self.dep)


class Tile:
    def __init__(self, h, name):
        self.h = h
        self.name = name
        self.d = Dep(name)
        self.parts = {}

    def __getitem__(self, idx):
        return View(self.h[idx], self.d)

    def k(self, key):
        d = self.parts.get(key)
        if d is None:
            d = self.parts[key] = Dep(f"{self.name}.{key}")
        return View(self.h, d)

    def v(self):
        return View(self.h[:], self.d)


ENGS = ("pe", "act", "dve", "pool", "sp")


class Prog:
    def __init__(self, same_engine_sync=True):
        self.nc = bass.Bass("TRN2", target_bir_lowering=False)
        nc = self.nc
        self.q = {e: [] for e in ENGS}
        self.cnt = {e: 0 for e in ENGS}
        self.seen = {e: {} for e in ENGS}
        self.sems = {}
        for e in ("pe", "act", "dve", "pool"):
            self.sems[e] = nc.alloc_semaphore(f"prog_{e}")
        self.ndma = 0
        self.dma_final = {}
        self.same = same_engine_sync
        self.relax_waw = True
        import os
        self.psum_guard = bool(int(os.environ.get('KGUARD', '0')))
        self.n_sb = 0
        self.banks = []
        self.bank_i = 0
        self.ninst = 0

    def sb(self, shape, dtype=F32, name=None):
        self.n_sb += 1
        name = "s_" + (name or f"sb{self.n_sb}")
        return Tile(self.nc.alloc_sbuf_tensor(name, list(shape), dtype), name)

    def make_banks(self, n=8):
        for i in range(n):
            h = self.nc.alloc_psum_tensor(f"bank{i}", [128, 512], F32)
            t = Tile(h, f"bank{i}")
            t.d.psum = True
            self.banks.append(t)
        self.junk = self.sb([128, 4], F32, "junk")

    def bank(self):
        b = self.banks[self.bank_i % getattr(self, "gen_banks", len(self.banks))]
        self.bank_i += 1
        return b

    def dram(self, name, shape, dtype=F32, kind="Internal"):
        h = self.nc.dram_tensor(name, list(shape), dtype, kind=kind)
        return Tile(h.ap() if hasattr(h, "ap") else h, name)

    def _waits(self, eng, outs, ins):
        w = {}
        seen = self.seen[eng]

        def need(ev):
            if ev is None:
                return
            k, v = ev
            if k == eng and (eng == "pe" or not self.same):
                return
            if seen.get(k, 0) >= v:
                return
            if w.get(k, 0) < v:
                w[k] = v

        for x in ins:
            need(x.dep.lw)
            if x.dep.pend:
                for ev in x.dep.pend.items():
                    need(ev)
            if x.dep.psum:
                for k, v in x.dep.rd.items():
                    if k != eng:
                        need((k, v))
        for x in outs:
            if not (self.relax_waw and x.dep.lw is not None and x.dep.lw[0] == eng):
                need(x.dep.lw)
            for k, v in x.dep.rd.items():
                need((k, v))
        for k, v in w.items():
            seen[k] = v
        return list(w.items())

    def op(self, eng, fn, outs, ins):
        waits = self._waits(eng, outs, ins)
        self.cnt[eng] += 1
        seq = self.cnt[eng]
        self.q[eng].append((waits, fn, (eng, 1)))
        for x in ins:
            x.dep.rd[eng] = seq
        for x in outs:
            x.dep.lw = (eng, seq)
            x.dep.rd = {}
        self.ninst += 1
        if self.psum_guard and eng in ("act", "dve") and any(x.dep.psum for x in ins):
            self.cnt[eng] += 1
            seq2 = self.cnt[eng]
            w2 = []
            if self.seen[eng].get(eng, 0) < seq:
                w2 = [(eng, seq)]
                self.seen[eng][eng] = seq
            j = self.junk.h
            if eng == "act":
                fn2 = lambda e: e.activation(j[0:1, 1:2], j[0:1, 0:1], AF.Identity)
            else:
                fn2 = lambda e: e.tensor_copy(j[0:1, 3:4], j[0:1, 2:3])
            self.q[eng].append((w2, fn2, (eng, 1)))
            for x in ins:
                if x.dep.psum:
                    x.dep.rd[eng] = seq2
            self.ninst += 1

    def dma(self, eng, out, in_, semdep=None, **kw):
        waits = self._waits(eng, [out], [in_])
        d = semdep or (in_.dep if eng == "pool" else out.dep)
        if eng == "pool":
            if d.sem2 is None:
                self.ndma += 1
                d.sem2 = f"dmasw{self.ndma}"
                self.sems[d.sem2] = self.nc.alloc_semaphore(d.sem2)
            d.cnt2 += 16
            skey, sval = d.sem2, d.cnt2
        else:
            if d.sem is None:
                self.ndma += 1
                d.sem = f"dma{self.ndma}"
                self.sems[d.sem] = self.nc.alloc_semaphore(d.sem)
            d.cnt += 16
            skey, sval = d.sem, d.cnt
        ev = (skey, sval)
        self.dma_final[skey] = sval
        oa, ia = out.ap, in_.ap
        self.q[eng].append((waits, lambda e: e.dma_start(out=oa, in_=ia, **kw), (skey, 16)))
        in_.dep.rd[skey] = sval
        if (semdep is not None and semdep is not out.dep) or eng == "pool":
            if out.dep.pend is None:
                out.dep.pend = {}
            out.dep.pend[skey] = sval
        else:
            out.dep.lw = ev
            out.dep.rd = {}
        self.ninst += 1

    def allreduce(self, out, in_, groups, eng="pool"):
        return self.allgather(out, in_, groups, eng=eng, kind="AllReduce", op=mybir.AluOpType.add)

    def allgather(self, out, in_, groups, eng="pool", kind="AllGather", op=mybir.AluOpType.bypass):
        waits = self._waits(eng, [out], [in_])
        d = out.dep
        if d.sem is None:
            self.ndma += 1
            d.sem = f"dma{self.ndma}"
            self.sems[d.sem] = self.nc.alloc_semaphore(d.sem)
        import os
        inc = int(os.environ.get("KCCINC", "1"))
        d.cnt += inc
        self.dma_final[d.sem] = d.cnt
        oa, ia = out.ap.opt(), in_.ap.opt()
        self.q[eng].append((waits, lambda e: e.collective_compute(
            kind, op, replica_groups=groups, ins=[ia], outs=[oa]), (d.sem, inc)))
        in_.dep.rd[d.sem] = d.cnt
        out.dep.lw = (d.sem, d.cnt)
        out.dep.rd = {}
        self.ninst += 1

    def finish(self, final_eng="sp"):
        nc = self.nc
        fin = []
        for k, v in self.dma_final.items():
            fin.append((k, v))
        for e in ("pe", "act", "dve", "pool"):
            if self.cnt[e]:
                fin.append((e, self.cnt[e]))
        self.q[final_eng].append((fin, None, None))
        engobj = {"pe": "tensor", "act": "scalar", "dve": "vector", "pool": "gpsimd", "sp": "sync"}
        sems = self.sems
        with nc.Block() as block:
            for ename in ENGS:
                lst = self.q[ename]
                if not lst:
                    continue

                def body(e, lst=lst):
                    for waits, fn, inc in lst:
                        for k, v in waits:
                            e.wait_ge(sems[k], v)
                        if fn is not None:
                            ins = fn(e)
                            ins.then_inc(sems[inc[0]], inc[1])

                getattr(block, engobj[ename])(body)
        return nc

    def mm(self, out, lhsT, rhs, start=True, stop=True):
        o, l, r = out.ap, lhsT.ap, rhs.ap
        self.op("pe", lambda e: e.matmul(o, l, r, start=start, stop=stop), [out], [lhsT, rhs])

    def tr(self, out, in_, ident):
        o, i, d = out.ap, in_.ap, ident.ap
        self.op("pe", lambda e: e.transpose(o, i, d), [out], [in_, ident])

    def act(self, out, in_, func, bias=None, scale=None, accum=None, eng="act"):
        kw = {}
        ins = [in_]
        outs = [out]
        if bias is not None:
            if isinstance(bias, View):
                kw["bias"] = bias.ap
                ins.append(bias)
            else:
                kw["bias"] = float(bias)
        if scale is not None:
            if isinstance(scale, View):
                kw["scale"] = scale.ap
                ins.append(scale)
            else:
                kw["scale"] = float(scale)
        if accum is not None:
            kw["accum_out"] = accum.ap
            outs.append(accum)
        o, i = out.ap, in_.ap
        if not hasattr(self, "actlog"):
            self.actlog = []
        self.actlog.append(str(func).split(".")[-1])
        self.op(eng, lambda e: e.activation(o, i, func, **kw), outs, ins)

    def tt(self, out, a, b, op, eng="dve"):
        o, x, y = out.ap, a.ap, b.ap
        self.op(eng, lambda e: e.tensor_tensor(o, x, y, op), [out], [a, b])

    def ts(self, out, a, s1, op0, s2=None, op1=None, eng="dve", accum=None):
        ins = [a]
        outs = [out]
        v1 = s1.ap if isinstance(s1, View) else float(s1)
        if isinstance(s1, View):
            ins.append(s1)
        v2 = None
        if s2 is not None:
            v2 = s2.ap if isinstance(s2, View) else float(s2)
            if isinstance(s2, View):
                ins.append(s2)
        o, x = out.ap, a.ap
        kw = {}
        if op1 is not None:
            kw["op1"] = op1
        if accum is not None:
            kw["accum_out"] = accum.ap
            outs.append(accum)
        self.op(eng, lambda e: e.tensor_scalar(o, x, v1, v2, op0, **kw), outs, ins)

    def stt(self, out, a, s, b, op0, op1, eng="dve", accum=None):
        ins = [a, b]
        outs = [out]
        sv = s.ap if isinstance(s, View) else float(s)
        if isinstance(s, View):
            ins.append(s)
        o, x, y = out.ap, a.ap, b.ap
        kw = {}
        if accum is not None:
            kw["accum_out"] = accum.ap
            outs.append(accum)
        self.op(eng, lambda e: e.scalar_tensor_tensor(o, x, sv, y, op0, op1, **kw), outs, ins)

    def cp(self, out, in_, eng="dve"):
        o, i = out.ap, in_.ap
        if eng == "act":
            self.op(eng, lambda e: e.activation(o, i, AF.Identity), [out], [in_])
        else:
            self.op(eng, lambda e: e.tensor_copy(o, i), [out], [in_])

    def memset(self, out, val, eng="pool"):
        o = out.ap
        self.op(eng, lambda e: e.memset(o, val), [out], [])

    def scan(self, out, d0, d1, init, op0, op1, eng="dve"):
        o, a, b = out.ap, d0.ap, d1.ap
        ins = [d0, d1]
        iv = init.ap if isinstance(init, View) else float(init)
        if isinstance(init, View):
            ins.append(init)
        self.op(eng, lambda e: e.tensor_tensor_scan(o, a, b, iv, op0, op1), [out], ins)

    def reduce(self, out, in_, op, axis=AX.X, eng="dve"):
        o, i = out.ap, in_.ap
        self.op(eng, lambda e: e.tensor_reduce(o, i, axis, op), [out], [in_])

    def recip(self, out, in_):
        o, i = out.ap, in_.ap
        self.op("dve", lambda e: e.reciprocal(o, i), [out], [in_])


import ml_dtypes

D = 1024
TT = 512
NG = 17
EPS = 1e-6
GN_EPS = 64e-5
DECAY_SCALE = 0.606531
QSCALE = 192.0 ** -0.5
ST_ENG = "pool"
import os as _os
KSUB = int(_os.environ.get('KSUB', '99'))
KSUB2 = int(_os.environ.get('KSUB2', '99'))
KDBG = int(_os.environ.get('KDBG', '0'))
KCH = int(_os.environ.get('KCH', '6'))
KATT = int(_os.environ.get('KATT', '-1'))
KKT = int(_os.environ.get('KKT', '-1'))
KSKIP = int(_os.environ.get('KSKIP', '0'))
KSQ = int(_os.environ.get('KSQ', '1'))

_CST_ITEMS = [("ident", 128), ("ones", 128), ("tri", 128), ("mst", 128), ("mit", 128), ("ms", 128),
              ("bd", 128), ("half", 128), ("hind", 2), ("rm128", 512), ("rm64", 512), ("invf", 1), ("sgn", 1)]
CST = {}
_o = 0
for _n, _w in _CST_ITEMS:
    CST[_n] = (_o, _w)
    _o += _w
NCST = _o

_PC_ITEMS = [("normw", 8), ("shiftb", 8), ("scaleb", 8), ("cvec", 8), ("qnw", 2), ("kvnw", 1),
             ("cw_xs", 4), ("cw_B", 4), ("cw_C", 4), ("cb_xs", 1), ("cb_B", 1), ("cb_C", 1),
             ("ssdnw", 1), ("dskip", 1), ("dtb0", 1), ("dtb1", 1), ("alog0", 1), ("alog1", 1),
             ("mu_r", 1), ("mu_k", 1), ("mu_v", 1), ("mu_lo", 1), ("w0", 1), ("a0", 1), ("kk", 1),
             ("ka", 1), ("rk", 1)]
PC = {}
_o = 0
for _n, _w in _PC_ITEMS:
    PC[_n] = (_o, _w)
    _o += _w
NPC = _o


def consts_array():
    c = np.zeros((128, NCST), np.float32)
    p = np.arange(128)[:, None]
    f = np.arange(128)[None, :]

    def put(name, a):
        o, w = CST[name]
        c[:, o:o + w] = a

    put("ident", (p == f))
    put("ones", np.ones((128, 128)))
    put("tri", (f >= p))
    same = (p // 64) == (f // 64)
    put("mst", same & (p < f))
    put("mit", same & (p <= f))
    put("ms", same & (f < p))
    put("bd", same)
    put("half", np.full((128, 128), 0.5))
    put("hind", (p // 64) == np.arange(2)[None, :])
    cc = np.arange(512)[None, :]
    put("rm128", np.broadcast_to((cc % 128 != 0), (128, 512)))
    put("rm64", np.broadcast_to((cc % 64 != 0), (128, 512)))
    invf = (10000.0 ** (-(np.arange(32, dtype=np.float32)) / 32.0)).astype(np.float32)
    iv = np.concatenate([invf, invf, invf, invf]).reshape(128, 1)
    put("invf", iv)
    sg = np.ones((128, 1), np.float32)
    sg[0:32] = -1.0
    sg[64:96] = -1.0
    put("sgn", sg)
    return c


O_QLAT, O_KVLAT, O_KPE, O_GMLA, O_Z, O_XS, O_B, O_C, O_DT = 0, 256, 384, 448, 960, 1472, 1984, 2240, 2496
O_R, O_K, O_V, O_WLO, O_ALO, O_GRW = 2504, 3016, 3528, 4040, 4104, 4168


def _col8(v):
    return np.ascontiguousarray(np.asarray(v, np.float32).reshape(8, 128).T)


def prep_A(inp, l, b, g, S):
    f32 = np.float32
    w_in = inp["w_in"][l]
    grp = g // 2
    cols = []
    cols.append(np.arange(O_QLAT, O_QLAT + 128))
    cols.append(np.arange(O_QLAT + 128, O_QLAT + 256))
    cols.append(np.arange(O_KVLAT, O_KVLAT + 128))
    kpe = np.arange(O_KPE, O_KPE + 64)
    cols.append(np.concatenate([kpe, kpe]))
    cols.append(np.arange(O_GMLA + 128 * g, O_GMLA + 128 * g + 128))
    cols.append(np.arange(O_Z + 128 * g, O_Z + 128 * g + 128))
    cols.append(np.arange(O_XS + 128 * g, O_XS + 128 * g + 128))
    cols.append(np.arange(O_B + 128 * grp, O_B + 128 * grp + 128))
    cols.append(np.arange(O_C + 128 * grp, O_C + 128 * grp + 128))
    cols.append(np.full(128, O_DT + 2 * g))
    cols.append(np.full(128, O_DT + 2 * g + 1))
    cols.append(np.arange(O_R + 128 * g, O_R + 128 * g + 128))
    cols.append(np.arange(O_K + 128 * g, O_K + 128 * g + 128))
    cols.append(np.arange(O_V + 128 * g, O_V + 128 * g + 128))
    cols.append(np.concatenate([np.arange(O_WLO, O_WLO + 64), np.arange(O_ALO, O_ALO + 64)]))
    cols.append(np.arange(O_GRW + 128 * g, O_GRW + 128 * g + 128))
    cols.append(np.concatenate([kpe[32:], kpe[:32], kpe[32:], kpe[:32]]))
    cols = np.concatenate(cols)
    wcat = np.ascontiguousarray(w_in[:, cols])

    pc = np.zeros((128, NPC), f32)

    def put(name, a):
        o, w = PC[name]
        pc[:, o:o + w] = np.asarray(a, f32).reshape(128, w)

    put("normw", _col8(inp["norm_w"][l]))
    put("shiftb", _col8(inp["ada_b"][l][0:1024]))
    put("scaleb", _col8(inp["ada_b"][l][1024:2048]))
    put("cvec", _col8(inp["c"][b]))
    put("qnw", np.asarray(inp["q_norm_w"][l]).reshape(2, 128).T)
    put("kvnw", inp["kv_norm_w"][l])
    cw = inp["conv_w"][l]
    cb = inp["conv_b"][l]
    sl_xs = slice(128 * g, 128 * g + 128)
    sl_B = slice(512 + 128 * grp, 512 + 128 * grp + 128)
    sl_C = slice(768 + 128 * grp, 768 + 128 * grp + 128)
    put("cw_xs", cw[:, sl_xs].T)
    put("cw_B", cw[:, sl_B].T)
    put("cw_C", cw[:, sl_C].T)
    put("cb_xs", cb[sl_xs])
    put("cb_B", cb[sl_B])
    put("cb_C", cb[sl_C])
    put("ssdnw", inp["ssd_norm_w"][l][128 * g:128 * g + 128])
    put("dskip", np.repeat(inp["d_skip"][l][2 * g:2 * g + 2], 64))
    put("dtb0", np.full(128, inp["dt_bias"][l][2 * g]))
    put("dtb1", np.full(128, inp["dt_bias"][l][2 * g + 1]))
    put("alog0", np.full(128, inp["a_log"][l][2 * g]))
    put("alog1", np.full(128, inp["a_log"][l][2 * g + 1]))
    hs = slice(128 * g, 128 * g + 128)
    put("mu_r", inp["mu_rkv"][l][0][hs])
    put("mu_k", inp["mu_rkv"][l][1][hs])
    put("mu_v", inp["mu_rkv"][l][2][hs])
    put("mu_lo", np.concatenate([inp["mu_w"][l], inp["mu_a"][l]]))
    put("w0", inp["w0"][l][hs])
    put("a0", inp["a0"][l][hs])
    put("kk", inp["k_k"][l][hs])
    put("ka", inp["k_a"][l][hs])
    put("rk", np.asarray(inp["r_k"][l]).reshape(512)[hs])

    wuq = inp["w_uq"][l][:, 192 * g:192 * g + 192]
    pe = wuq[:, 128:192]
    pes = np.concatenate([pe[:, 32:], pe[:, :32]], axis=1)
    wuq_c = np.concatenate([wuq[:, 0:128], pe, pe, pes, pes], axis=1)
    wukv_c = inp["w_ukv"][l][:, 256 * g:256 * g + 256]
    z64 = np.zeros((64, 128), np.float32)
    lora = np.concatenate([np.concatenate([inp["w_lora_b"][l][:, hs], z64], axis=0),
                           np.concatenate([z64, inp["a_lora_b"][l][:, hs]], axis=0)], axis=1)
    lnrow = np.concatenate([np.broadcast_to(inp["lnx_w"][l][hs], (128, 128)),
                            np.broadcast_to(inp["lnx_b"][l][hs], (128, 128))], axis=1)
    return {
        "wcat": wcat.astype(f32),
        "adaw": np.ascontiguousarray(inp["ada_w"][l][:, 0:2048]).astype(f32),
        "pcol": pc,
        "wuq": np.ascontiguousarray(wuq_c).astype(f32),
        "wukv": np.ascontiguousarray(wukv_c).astype(f32),
        "lora": np.ascontiguousarray(lora).astype(f32),
        "lnrow": np.ascontiguousarray(lnrow).astype(f32),
        "pos": np.ascontiguousarray(inp["positions"][b][None, :S]).astype(np.int32),
        "cst": consts_array(),
    }


class PhaseA:
    def __init__(self, P, S, x, y_loc, ssq_out, io, tag=""):
        self.P = P
        self.S = S
        self.x = x
        self.y_loc = y_loc
        self.ssq_out = ssq_out
        self.io = io
        self.tag = tag

    def c(self, name, rows=slice(0, 128)):
        o, w = CST[name]
        return self.cst[rows, o:o + w]

    def bfv(self, name):
        t = self.W[name]
        return View(t.h[:].bitcast(BF16)[:, 0:TT], t.d)

    def ydst(self, br, tok0):
        if isinstance(self.y_loc, list):
            return self.y_loc[tok0 // 1024][br * 128:(br + 1) * 128, tok0 % 1024:tok0 % 1024 + TT]
        return self.y_loc[br * 128:(br + 1) * 128, tok0:tok0 + TT]

    def pc(self, name, j=0, rows=slice(0, 128)):
        o, w = PC[name]
        return self.pcol[rows, o + j:o + j + 1]

    def setup(self):
        P, io = self.P, self.io
        S = self.S
        if not hasattr(self, "_tiles"):
            self._tiles = {}

        def sb(shape, dt=F32, name=None):
            t = self._tiles.get(name)
            if t is None:
                t = self._tiles[name] = P.sb(shape, dt, name)
            return t

        self.cst = sb([128, NCST], F32, "cst" + self.tag)
        P.dma("sp", self.cst.v(), io["cst"].v())
        self.pcol = sb([128, NPC], F32, "pcol" + self.tag)
        P.dma("sp", self.pcol.v(), io["pcol"].v())
        self.lnrow = sb([128, 256], F32, "lnrow" + self.tag)
        P.dma("sp", self.lnrow.v(), io["lnrow"].v())
        W = {}

        def wt(name, shape, dt=F32):
            W[name] = sb(shape, dt, name + self.tag)
            return W[name]

        self.xt = [wt(f"xt{i}", [128, 1024]) for i in range(2)]
        self.xn = [wt("xn0", [128, 1024])]
        stg = [self.xt[0], self.xt[1], self.xn[0]]
        NW = NG * 128
        self.w_sb = sb([128, 8, NW], BF16, "w_sb" + self.tag)
        wv = io["wcat"].v().re("(c p) n -> p c n", p=128)
        n = 0
        for kc in range(8):
            for c0 in range(0, NW, 1024):
                c1 = min(NW, c0 + 1024)
                s_ = stg[n % 3]
                n += 1
                P.dma("sp", s_[:, 0:c1 - c0], wv[:, kc, c0:c1])
                P.cp(self.w_sb[:, kc, c0:c1], s_[:, 0:c1 - c0], eng=("dve", "act")[n % 2])
        wq = self.xn[0]
        P.dma("sp", wq[:, 0:768].re("p (c n) -> p c n", c=2), io["wuq"].v().re("(c p) n -> p c n", p=128))
        self.wuq = sb([128, 2, 384], BF16, "wuq" + self.tag)
        P.cp(self.wuq.v(), wq[:, 0:768].re("p (c n) -> p c n", c=2))
        wk = self.xt[0]
        P.dma("sp", wk[:, 0:256], io["wukv"].v())
        self.wukv = sb([128, 256], BF16, "wukv" + self.tag)
        P.cp(self.wukv.v(), wk[:, 0:256])
        self.cbf = sb([128, 4, 128], BF16, "cbf" + self.tag)
        for j_, nm_ in enumerate(("ident", "ones", "half", "bd")):
            P.cp(self.cbf[:, j_, :], self.c(nm_))
        self.xnb = sb([128, 1024], BF16, "xnb" + self.tag)
        self.lora = sb([128, 256], F32, "lora" + self.tag)
        P.dma("sp", self.lora.v(), io["lora"].v())

        self.dc = sb([128, 64], F32, "dcols" + self.tag)
        dc = self.dc
        self.cact = dc[:, 0:8]
        P.act(self.cact, self.pcol[:, PC["cvec"][0]:PC["cvec"][0] + 8], AF.Silu)
        bk = P.bank()
        aw = io["adaw"].v().re("(c p) n -> p c n", p=128)
        st4 = [self.xt[0], self.xt[1]]
        for piece in range(8):
            for kc in range(8):
                dst = st4[kc // 4][:, (kc % 4) * 256:(kc % 4) * 256 + 256]
                P.dma("sp", dst, aw[:, kc, piece * 256:(piece + 1) * 256])
            for fcl in range(2):
                fc = piece * 2 + fcl
                for kc in range(8):
                    src = st4[kc // 4][:, (kc % 4) * 256 + fcl * 128:(kc % 4) * 256 + fcl * 128 + 128]
                    P.mm(bk[:, fc:fc + 1], src, self.cact[:, kc:kc + 1], start=(kc == 0), stop=(kc == 7))
        self.shift = dc[:, 8:16]
        self.Acol = dc[:, 16:24]
        o_sh, o_sc, o_nw = PC["shiftb"][0], PC["scaleb"][0], PC["normw"][0]
        P.tt(self.shift, bk[:, 0:8], self.pcol[:, o_sh:o_sh + 8], ALU.add)
        tmp8 = dc[:, 24:32]
        P.tt(tmp8, bk[:, 8:16], self.pcol[:, o_sc:o_sc + 8], ALU.add)
        P.stt(self.Acol, tmp8, 1.0, self.pcol[:, o_nw:o_nw + 8], ALU.add, ALU.mult)
        self.ommu = dc[:, 32:36]
        o_mu = PC["mu_r"][0]
        P.ts(self.ommu, self.pcol[:, o_mu:o_mu + 4], -1.0, ALU.mult, 1.0, ALU.add)
        self.omka = dc[:, 36:37]
        P.ts(self.omka, self.pc("ka"), -1.0, ALU.mult, 1.0, ALU.add)
        self.ah = dc[:, 37:39]
        o_al = PC["alog0"][0]
        P.act(dc[:, 40:42], self.pcol[:, o_al:o_al + 2], AF.Exp)
        P.ts(self.ah, dc[:, 40:42], -1.0, ALU.mult)
        self.kmax2 = dc[:, 42:43]
        P.memset(self.kmax2, 0.0)

        self.KA = sb([128, S], BF16, "KA" + self.tag)
        self.KB = sb([128, S // 2], BF16, "KB" + self.tag)
        nt = S // 128
        self.V = sb([128, nt, 132], BF16, "V" + self.tag)
        P.memset(self.V[:, :, 129:132], 0.0)
        P.memset(self.V[:, :, 128:129], 1.0)
        self.cbuf = [sb([128, 3 + TT], F32, f"cbuf{i}" + self.tag) for i in range(3)]
        for t in self.cbuf:
            P.memset(t[:, 0:3], 0.0)
        self.sbuf = [sb([128, 1 + TT], F32, f"shb{i}" + self.tag) for i in range(4)]
        for t in self.sbuf:
            P.memset(t[:, 0:1], 0.0)
        self.prev = sb([128, 128], F32, "ssd_prev" + self.tag)
        P.memset(self.prev.v(), 0.0)
        self.prev_bf = sb([128, 128], BF16, "ssd_prevbf" + self.tag)
        P.memset(self.prev_bf.v(), 0.0)
        self.Sbd = [sb([128, 128], F32, f"Sbd{i}" + self.tag) for i in range(2)]
        P.memset(self.Sbd[0].v(), 0.0)
        self.sbd_i = 0
        self.rh0 = [sb([128, 128], F32, f"rh0_{i}" + self.tag) for i in range(2)]
        self.rh1 = [sb([128, 128], F32, f"rh1_{i}" + self.tag) for i in range(2)]
        for t in self.rh0:
            P.memset(t[:, 64:128], 0.0)
        for t in self.rh1:
            P.memset(t[:, 0:64], 0.0)

        wt("stat", [128, 16])
        wt("hT", [128, 8, TT], BF16)
        G = [wt(f"G{i}", [128, TT]) for i in range(15)]
        for i in range(6):
            wt(f"t{i}", [128, TT])
        for j, n_ in enumerate(["r_s", "k_s", "v_s", "lo_s", "sg", "logw", "a_s", "kkn", "kp", "lw", "rt", "kt", "bt", "at"]):
            W[n_] = G[j]
        for j, n_ in enumerate(["sz", "xsT", "BT", "CT", "dtb0", "dtb1", "ac0", "ac1"]):
            W[n_] = G[j]
        for j, n_ in enumerate(["qlat0", "qlat1", "kvlat", "sqA", "sqB", "rstd", "gm"]):
            W[n_] = G[8 + j]
        W["CC"] = G[0]
        W["SS"] = G[1]
        W["posi"] = G[2]
        W["ki"] = G[3]
        wt("qn", [128, 2, TT], BF16)
        wt("kvn", [128, TT], BF16)
        wt("QA", [128, TT], BF16)
        wt("QB", [128, TT], BF16)
        wt("BTb", [128, TT], BF16)
        wt("CTb", [128, TT], BF16)
        self.PT = [wt(f"PT{i}", [128, TT], BF16) for i in range(3)]
        wt("yout", [128, TT], BF16)
        wt("yout2", [128, TT], BF16)
        wt("yout3", [128, TT], BF16)
        wt("ssqr", [1, TT])
        self.W = W
        self.G = G
        if not hasattr(self, "scr"):
            self.scr = {}

    def sc(self, name, shape=(128, 128), dt=F32, n=1):
        key = name
        ent = self.scr.get(key)
        if ent is None:
            ent = self.scr[key] = [[self.P.sb(list(shape), dt, f"{name}_{i}" + self.tag) for i in range(n)], 0]
        t = ent[0][ent[1] % n]
        ent[1] += 1
        return t

    def run(self):
        self.setup()
        import os
        if int(os.environ.get("KSTAGE", "9")) < 1:
            return
        self.run_tiles()

    def ht_gen(self, i):
        P, W = self.P, self.W
        tok0 = i * TT
        hT = W["hT"]
        stat = W["stat"]
        for sub in range(4):
            xt = self.xt[sub % 2]
            xn = self.xn[0]
            P.dma("sp", xt.v(), self.x[tok0 + sub * 128: tok0 + sub * 128 + 128, :])
            ssq = stat[:, sub:sub + 1]
            P.act(xn.v(), xt.v(), AF.Square, accum=ssq)
            rs = stat[:, 4 + sub:5 + sub]
            P.act(rs, ssq, AF.Sqrt, scale=1.0 / D, bias=EPS)
            rr = stat[:, 8 + sub:9 + sub]
            P.recip(rr, rs)
            xnb = self.xnb
            P.ts(xnb.v(), xt.v(), rr, ALU.mult)
            yield
            bk = P.bank()
            bkb = View(bk.h[:].bitcast(BF16), bk.d)
            identb = self.cbf[:, 0, :]
            for fc in range(8):
                P.tr(bkb[:, fc * 128:(fc + 1) * 128], xnb[:, fc * 128:(fc + 1) * 128], identb)
            for fc in range(8):
                P.act(hT[:, fc, sub * 128:(sub + 1) * 128], bkb[:, fc * 128:(fc + 1) * 128], AF.Identity,
                      bias=self.shift[:, fc:fc + 1], scale=self.Acol[:, fc:fc + 1])
            yield

    def inproj(self, gi, m0=0, m1=128):
        P = self.P
        hT = self.W["hT"]
        bk = P.bank()
        for kc in range(8):
            P.mm(bk[0:m1 - m0, :], self.w_sb[:, kc, gi * 128 + m0: gi * 128 + m1], hT[:, kc, :],
                 start=(kc == 0), stop=(kc == 7))
        return bk

    @staticmethod
    def interleave(gens):
        gens = [g for g in gens if g is not None]
        while gens:
            for g in list(gens):
                try:
                    next(g)
                except StopIteration:
                    gens.remove(g)

    def run_tiles(self):
        nt = self.S // TT
        ht_done = False
        for i in range(nt):
            if not ht_done:
                self.interleave([self.ht_gen(i)])
            self.P.gen_banks = 8
            pb = self.mla_proj(i)
            self.rope_tables(i)
            self.mla_pre(i, pb)
            self.P.gen_banks = 4
            att = self.att_gen(i)
            rest = self.rest_gen(i, nt)
            att_alive = rest_alive = True
            while att_alive or rest_alive:
                if att_alive:
                    try:
                        next(att)
                    except StopIteration:
                        att_alive = False
                        self.P.gen_banks = 8
                if rest_alive:
                    try:
                        next(rest)
                    except StopIteration:
                        rest_alive = False
            ht_done = (i + 1 < nt)

    def rest_gen(self, i, nt):
        yield from self.ssd_gen(i)
        g = self.rwkv_gen(i)
        next(g)
        yield
        if i + 1 < nt:
            gs = [g, self.ht_gen(i + 1)]
            while gs:
                for x_ in list(gs):
                    try:
                        next(x_)
                    except StopIteration:
                        gs.remove(x_)
                yield
        else:
            yield from g

    def rope_tables(self, i):
        import os
        P, W = self.P, self.W
        tok0 = i * TT
        CC, SS = W["CC"], W["SS"]
        posi = View(W["posi"].h[:].bitcast(I32), W["posi"].d)
        ki = View(W["ki"].h[:].bitcast(I32), W["ki"].d)
        t0, t1, t2 = W["t0"], W["t1"], W["t2"]
        rc = getattr(self, "rope_cache", None)
        if rc is not None and getattr(self, "layer", 0) == 1:
            P.dma("sp", CC.v(), rc[0, :, tok0:tok0 + TT])
            P.dma("sp", SS.v(), rc[1, :, tok0:tok0 + TT])
            return
        pv = self.io["pos"]
        P.dma("sp", posi, View(pv.h[0:1, tok0:tok0 + TT].broadcast_to([128, TT]), pv.d))
        KR = int(os.environ.get("KROPE", "9")) if i == 3 else 9
        if KR < 2:
            return
        ang = t0.v()
        P.cp(ang, posi)
        P.ts(ang, ang, self.c("invf"), ALU.mult)
        TWO_PI = 6.283185307179586
        C1 = 6.28125
        C2 = TWO_PI - C1
        if KR < 3:
            return
        for which, out in ((0, SS), (1, CC)):
            a2 = t1.v()
            if which == 1:
                P.ts(a2, ang, 1.5707963267948966, ALU.add)
            else:
                P.cp(a2, ang)
            kf = t2.v()
            P.ts(kf, a2, 1.0 / TWO_PI, ALU.mult)
            if KR < 4:
                continue
            P.cp(ki, kf)
            P.cp(kf, ki)
            P.stt(a2, kf, -C1, a2, ALU.mult, ALU.add)
            P.stt(a2, kf, -C2, a2, ALU.mult, ALU.add)
            P.ts(a2, a2, 3.1415925, ALU.min, -3.1415925, ALU.max)
            if KR < 5:
                continue
            P.act(out.v(), a2, AF.Sin)
        P.ts(SS.v(), SS.v(), self.c("sgn"), ALU.mult)
        if rc is not None:
            P.dma(ST_ENG, rc[0, :, tok0:tok0 + TT], CC.v())
            P.dma(ST_ENG, rc[1, :, tok0:tok0 + TT], SS.v())

    def rms_feat(self, banks, dsts, nw_cols, nfeat, out_bf):
        P, W = self.P, self.W
        ones = self.cbf[:, 1, :]
        sqs = [self.bfv("sqA"), self.bfv("sqB")]
        for j, bk in enumerate(banks):
            P.act(dsts[j].v(), bk, AF.Identity)
            P.act(sqs[j], bk, AF.Square)
        sb_ = P.bank()
        for j in range(len(banks)):
            P.mm(sb_[:, :], ones, sqs[j], start=(j == 0), stop=(j == len(banks) - 1))
        rstd = W["rstd"]
        P.act(rstd.v(), sb_[:, :], AF.Sqrt, scale=1.0 / nfeat, bias=EPS)
        P.recip(rstd.v(), rstd.v())
        for j in range(len(banks)):
            P.stt(out_bf[j], dsts[j].v(), nw_cols[j], rstd.v(), ALU.mult, ALU.mult)

    def mla_proj(self, i):
        return {g_: self.inproj(g_) for g_ in (0, 1, 2, 4, 3, 16)}

    def mla_pre(self, i, pb):
        P, W = self.P, self.W
        S = self.S
        tok0 = i * TT
        ones = self.c("ones")
        ident = self.c("ident")
        b0 = pb[0]
        b1 = pb[1]
        qn = W["qn"]
        self.rms_feat([b0[:, :], b1[:, :]], [W["qlat0"], W["qlat1"]],
                      [self.pc("qnw", 0), self.pc("qnw", 1)], 256, [qn[:, 0, :], qn[:, 1, :]])
        b2 = pb[2]
        kvn = W["kvn"]
        self.rms_feat([b2[:, :]], [W["kvlat"]], [self.pc("kvnw")], 128, [kvn.v()])
        b4 = pb[4]
        P.act(W["gm"].v(), b4[:, :], AF.Silu)
        QA, QB = W["QA"], W["QB"]
        sqA, sqB = self.bfv("sqA"), self.bfv("sqB")
        onesb, halfb = self.cbf[:, 1, :], self.cbf[:, 2, :]
        CC, SS = W["CC"], W["SS"]
        t0, t1 = W["t0"], W["t1"]
        bq = P.bank()
        for c in range(2):
            P.mm(bq[:, :], self.wuq[:, c, 0:128], qn[:, c, :], start=(c == 0), stop=(c == 1))
        P.act(QA.v(), bq[:, :], AF.Identity, scale=QSCALE)
        P.act(sqA, bq[:, :], AF.Square, scale=QSCALE)
        bp = P.bank()
        for c in range(2):
            P.mm(bp[:, :], self.wuq[:, c, 128:256], qn[:, c, :], start=(c == 0), stop=(c == 1))
        bs = P.bank()
        for c in range(2):
            P.mm(bs[:, :], self.wuq[:, c, 256:384], qn[:, c, :], start=(c == 0), stop=(c == 1))
        P.act(sqB, bp[:, :], AF.Square, scale=QSCALE)
        P.tt(t0.v(), bp[:, :], CC.v(), ALU.mult)
        P.tt(t1.v(), bs[:, :], SS.v(), ALU.mult)
        P.tt(t0.v(), t0.v(), t1.v(), ALU.add)
        P.ts(QB.v(), t0.v(), QSCALE, ALU.mult)
        bqq = P.bank()
        P.mm(bqq[:, :], onesb, sqA, start=True, stop=False)
        P.mm(bqq[:, :], halfb, sqB, start=False, stop=True)
        qmx = self.dc[:, 44:45]
        P.reduce(qmx, bqq[:, :], ALU.max)
        bk3 = pb[3]
        bk3s = pb[16]
        P.act(sqB, bk3[:, :], AF.Square)
        P.tt(t0.v(), bk3[:, :], CC.v(), ALU.mult)
        P.tt(t1.v(), bk3s[:, :], SS.v(), ALU.mult)
        hlf = 0 if tok0 < S // 2 else 1
        kr = slice(64 * hlf, 64 * hlf + 64)
        kc0 = tok0 - hlf * (S // 2)
        P.tt(self.KB[kr, kc0:kc0 + TT], t0[kr, :], t1[kr, :], ALU.add)
        bkn = P.bank()
        P.mm(bkn[:, :], self.wukv[:, 0:128], kvn.v())
        P.act(self.KA[:, tok0:tok0 + TT], bkn[:, :], AF.Identity)
        P.act(sqA, bkn[:, :], AF.Square)
        bkk = P.bank()
        P.mm(bkk[:, :], onesb, sqA, start=True, stop=False)
        P.mm(bkk[:, :], halfb, sqB, start=False, stop=True)
        kmx = self.dc[:, 43:44]
        P.reduce(kmx, bkk[:, :], ALU.max)
        P.tt(self.kmax2, self.kmax2, kmx, ALU.max)
        bv = P.bank()
        for sub in range(4):
            P.mm(bv[:, sub * 128:(sub + 1) * 128], kvn[:, sub * 128:(sub + 1) * 128], self.wukv[:, 128:256])
        P.cp(self.V[:, 4 * i:4 * i + 4, 0:128], bv[:, :].re("p (s d) -> p s d", d=128))
        nb = self.dc[:, 45:46]
        P.tt(nb, qmx, self.kmax2, ALU.mult)
        P.act(nb, nb, AF.Sqrt)
        P.ts(nb, nb, -1.0, ALU.mult)
        return

    def att_gen(self, i):
        P, W = self.P, self.W
        S = self.S
        tok0 = i * TT
        ident = self.c("ident")
        QA, QB = W["QA"], W["QB"]
        nb = self.dc[:, 45:46]
        acc = P.banks[4:8]
        nk = 4 * i + 4
        for kt in range(nk):
            j = kt - 4 * i
            q0 = 0 if j < 0 else 128 * j
            khalf = 0 if kt * 128 < S // 2 else 1
            kr2 = slice(64 * khalf, 64 * khalf + 64)
            kk0 = kt * 128 - khalf * (S // 2)
            sbk = P.bank()
            P.mm(sbk[:, q0:TT], self.KA[:, kt * 128:(kt + 1) * 128], QA[:, q0:TT], start=True, stop=False)
            P.mm(sbk[:, q0:TT], self.KB[kr2, kk0:kk0 + 128], QB[kr2, q0:TT], start=False, stop=True)
            pt = self.PT[kt % 3]
            P.act(pt[:, q0:TT], sbk[:, q0:TT], AF.Exp, bias=nb)
            if j >= 0:
                P.memset(pt[64:128, q0:q0 + 64], 0.0)
            for sub in range(max(j, 0), 4):
                P.mm(acc[sub][:, 0:132], pt[:, sub * 128:(sub + 1) * 128], self.V[:, kt, :],
                     start=(kt == 0), stop=(kt == 4 * i + sub))
            yield
        yout = W["yout"]
        for sub in range(4):
            rinv = self.sc("rinv", (128, 1))
            P.recip(rinv.v(), acc[sub][:, 128:129])
            osb = self.sc("osb")
            P.ts(osb.v(), acc[sub][:, 0:128], rinv.v(), ALU.mult)
            tb = P.bank()
            P.tr(tb[:, 0:128], osb.v(), ident)
            P.tt(yout[:, sub * 128:(sub + 1) * 128], tb[:, 0:128], W["gm"][:, sub * 128:(sub + 1) * 128], ALU.mult)
            yield
        P.dma(ST_ENG, self.ydst(0, tok0), yout.v())

    def ssd_gen(self, i):
        inproj = self.inproj
        P, W = self.P, self.W
        tok0 = i * TT
        ident = self.c("ident")
        ones = self.c("ones")
        tri = self.c("tri")
        b5 = inproj(5)
        P.act(W["sz"].v(), b5[:, :], AF.Silu)
        outs = [W["xsT"], W["BT"], W["CT"]]
        for j, (gi, nm) in enumerate(((6, "xs"), (7, "B"), (8, "C"))):
            bk = inproj(gi)
            cb = self.cbuf[j]
            P.act(cb[:, 3:3 + TT], bk[:, :], AF.Identity)
            o_w = PC["cw_" + nm][0]
            acc = W["t0"]
            P.ts(acc.v(), cb[:, 3:3 + TT], self.pcol[:, o_w + 3:o_w + 4], ALU.mult, self.pc("cb_" + nm), ALU.add)
            for tap in range(3):
                P.stt(acc.v(), cb[:, tap:tap + TT], self.pcol[:, o_w + tap:o_w + tap + 1], acc.v(), ALU.mult, ALU.add)
            P.act(outs[j].v(), acc.v(), AF.Silu)
            P.cp(cb[:, 0:3], cb[:, TT:TT + 3])
            yield
        P.cp(W["BTb"].v(), W["BT"].v(), eng="pool")
        P.cp(W["CTb"].v(), W["CT"].v(), eng="pool")
        dtb = [W["dtb0"], W["dtb1"]]
        ac = [W["ac0"], W["ac1"]]
        for h in range(2):
            bk = inproj(9 + h)
            xm = W["t1"]
            P.ts(xm.v(), bk[:, :], self.pc("dtb%d" % h), ALU.add, 40.0, ALU.min)
            P.act(xm.v(), xm.v(), AF.Exp)
            P.act(xm.v(), xm.v(), AF.Ln, bias=1.0)
            P.stt(dtb[h].v(), bk[:, :], self.pc("dtb%d" % h), xm.v(), ALU.add, ALU.max)
            dta = W["t2"]
            P.ts(dta.v(), dtb[h].v(), self.ah[:, h:h + 1], ALU.mult)
            P.scan(ac[h].v(), self.c("rm128"), dta.v(), 0.0, ALU.mult, ALU.add)
            yield
        xsT, BT, CT, BTb, CTb = W["xsT"], W["BT"], W["CT"], W["BTb"], W["CTb"]
        ytok = W["t3"]
        yT = W["t4"]
        for cch in range(4):
            cs = slice(cch * 128, cch * 128 + 128)
            last = cch * 128 + 127
            bcb = P.bank()
            P.mm(bcb[:, 0:128], BTb[:, cs], CTb[:, cs])
            cbm = self.sc("cbm")
            P.tt(cbm.v(), bcb[:, 0:128], tri, ALU.mult)
            xdtT = self.sc("xdtTb", (128, 128), BF16)
            dte = self.sc("dte")
            for h in range(2):
                hp = slice(64 * h, 64 * h + 64)
                P.tt(xdtT[hp, :], xsT[hp, cs], dtb[h][hp, cs], ALU.mult)
                P.act(dte[hp, :], ac[h][hp, cs], AF.Exp, scale=-1.0, bias=ac[h][hp, last:last + 1])
            yield
            xdteT = self.sc("xdteTb", (128, 128), BF16)
            P.tt(xdteT.v(), xdtT.v(), dte.v(), ALU.mult)
            btr = P.bank()
            btrb = View(btr.h[:].bitcast(BF16), btr.d)
            identb = self.cbf[:, 0, :]
            P.tr(btrb[:, 0:128], xdtT.v(), identb)
            P.tr(btrb[:, 128:256], xdteT.v(), identb)
            P.tr(btrb[:, 256:384], BTb[:, cs], identb)
            tok3 = self.sc("tok3", (128, 384), BF16)
            P.act(tok3.v(), btrb[:, 0:384], AF.Identity)
            xdt, xdte, Btok = tok3[:, 0:128], tok3[:, 128:256], tok3[:, 256:384]
            yield
            by = P.bank()
            for h in range(2):
                hc = slice(64 * h, 64 * h + 64)
                acol = self.sc("acol", (128, 1))
                junk = self.sc("junk")
                P.stt(junk.v(), ac[h][:, cs], 1.0, ident, ALU.mult, ALU.mult, accum=acol.v())
                seg = self.sc("seg")
                P.ts(seg.v(), ac[h][:, cs], acol.v(), ALU.subtract, 0.0, ALU.min)
                P.act(seg.v(), seg.v(), AF.Exp)
                MT = self.sc("MT", (128, 128), BF16)
                P.tt(MT.v(), cbm.v(), seg.v(), ALU.mult, eng="pool")
                eac = self.sc("eac")
                P.act(eac.v(), ac[h][:, cs], AF.Exp)
                CpT = self.sc("CpT", (128, 128), BF16)
                P.tt(CpT.v(), CT[:, cs], eac.v(), ALU.mult)
                P.mm(by[:, hc], MT.v(), xdt[:, hc], start=True, stop=False)
                P.mm(by[:, hc], CpT.v(), self.prev_bf[:, hc], start=False, stop=True)
            yield
            bst = P.bank()
            P.mm(bst[:, 0:128], Btok, xdte)
            for h in range(2):
                hc = slice(64 * h, 64 * h + 64)
                cd = self.sc("cd", (128, 1))
                P.act(cd.v(), ac[h][:, last:last + 1], AF.Exp)
                P.stt(self.prev[:, hc], self.prev[:, hc], cd.v(), bst[:, hc], ALU.mult, ALU.add)
            P.cp(self.prev_bf.v(), self.prev.v())
            yield
            ysb = self.sc("ysb")
            P.act(ysb.v(), by[:, 0:128], AF.Identity)
            byT = P.bank()
            P.tr(byT[:, 0:128], ysb.v(), ident)
            P.stt(yT[:, cs], xsT[:, cs], self.pc("dskip"), byT[:, 0:128], ALU.mult, ALU.add)
        yield
        P.tt(yT.v(), yT.v(), W["sz"].v(), ALU.mult)
        sq = W["t5"]
        P.act(sq.v(), yT.v(), AF.Square)
        bsq = P.bank()
        P.mm(bsq[0:1, :], ones[:, 0:1], sq.v())
        P.cp(W["ssqr"].v(), bsq[0:1, :])
        P.dma(ST_ENG, self.ssq_out[0:1, tok0:tok0 + TT], W["ssqr"].v())
        P.ts(W["yout2"].v(), yT.v(), self.pc("ssdnw"), ALU.mult)
        P.dma(ST_ENG, self.ydst(1, tok0), W["yout2"].v())

    def rwkv_gen(self, i):
        inproj = self.inproj
        P, W = self.P, self.W
        tok0 = i * TT
        ident = self.c("ident")
        ones = self.c("ones")
        bd = self.c("bd")
        mst, mit, ms = self.c("mst"), self.c("mit"), self.c("ms")
        outs = [W["r_s"], W["k_s"], W["v_s"], W["lo_s"]]
        for j, gi in enumerate((11, 12, 13, 14)):
            bk = inproj(gi)
            sbf = self.sbuf[j]
            P.act(sbf[:, 1:1 + TT], bk[:, :], AF.Identity)
            tmp = W["t0"]
            P.ts(tmp.v(), sbf[:, 1:1 + TT], self.ommu[:, j:j + 1], ALU.mult)
            o_mu = PC["mu_r"][0] + j
            P.stt(outs[j].v(), sbf[:, 0:TT], self.pcol[:, o_mu:o_mu + 1], tmp.v(), ALU.mult, ALU.add)
            P.cp(sbf[:, 0:1], sbf[:, TT:TT + 1])
        bg = inproj(15)
        P.act(W["sg"].v(), bg[:, :], AF.Silu)
        yield
        r_s, k_s, v_s, lo_s = outs
        th = W["t0"]
        P.act(th.v(), lo_s.v(), AF.Tanh)
        bw = P.bank()
        P.mm(bw[:, :], self.lora[:, 0:128], th.v())
        logw = W["logw"]
        P.act(logw.v(), bw[:, :], AF.Sigmoid, bias=self.pc("w0"))
        P.ts(logw.v(), logw.v(), -DECAY_SCALE, ALU.mult)
        ba = P.bank()
        P.mm(ba[:, :], self.lora[:, 128:256], lo_s.v())
        a_s = W["a_s"]
        P.act(a_s.v(), ba[:, :], AF.Sigmoid, bias=self.pc("a0"))
        kkn = W["kkn"]
        P.ts(kkn.v(), k_s.v(), self.pc("kk"), ALU.mult)
        sq = self.bfv("t1")
        P.act(sq, kkn.v(), AF.Square)
        bss = P.bank()
        P.mm(bss[:, :], self.cbf[:, 3, :], sq)
        rn = W["t2"]
        P.act(rn.v(), bss[:, :], AF.Sqrt)
        P.ts(rn.v(), rn.v(), 1e-12, ALU.max)
        P.recip(rn.v(), rn.v())
        P.tt(kkn.v(), kkn.v(), rn.v(), ALU.mult)
        kp = W["kp"]
        P.ts(kp.v(), a_s.v(), self.pc("ka"), ALU.mult, self.omka, ALU.add)
        P.tt(kp.v(), kp.v(), k_s.v(), ALU.mult)
        yield
        lw = W["lw"]
        P.scan(lw.v(), self.c("rm64"), logw.v(), 0.0, ALU.mult, ALU.add)
        ew = W["t3"]
        P.act(ew.v(), lw.v(), AF.Exp)
        ewn = W["t4"]
        P.act(ewn.v(), lw.v(), AF.Exp, scale=-1.0)
        lx = W["t5"]
        P.tt(lx.v(), lw.v(), logw.v(), ALU.subtract)
        P.act(lx.v(), lx.v(), AF.Exp)
        rt, kt, bt, at = W["rt"], W["kt"], W["bt"], W["at"]
        P.tt(rt.v(), r_s.v(), ew.v(), ALU.mult)
        P.tt(kt.v(), kp.v(), ewn.v(), ALU.mult)
        P.tt(bt.v(), kkn.v(), a_s.v(), ALU.mult)
        P.tt(bt.v(), bt.v(), ewn.v(), ALU.mult)
        P.stt(at.v(), kkn.v(), -1.0, lx.v(), ALU.mult, ALU.mult)
        rk = W["t1"]
        P.stt(rk.v(), r_s.v(), self.pc("rk"), kp.v(), ALU.mult, ALU.mult)
        yield
        yrw = W["yout3"]
        for pr in range(4):
            ps = slice(pr * 128, pr * 128 + 128)
            btr = P.bank()
            P.tr(btr[:, 0:128], v_s[:, ps], ident)
            P.tr(btr[:, 128:256], (v_s if KDBG else bt)[:, ps], ident)
            P.tr(btr[:, 256:384], (v_s if KDBG else kt)[:, ps], ident)
            P.tr(btr[:, 384:512], (v_s if KDBG else at)[:, ps], ident)
            tk = self.sc("rtok", (128, 512))
            for q_ in range(4):
                P.cp(tk[:, q_ * 128:(q_ + 1) * 128], btr[:, q_ * 128:(q_ + 1) * 128])
            Vtok, Btok, Ktok, Atok = tk[:, 0:128], tk[:, 128:256], tk[:, 256:384], tk[:, 384:512]
            Zc = self.sc("Zc", (128, 2, 128))
            P.cp(Zc[:, :, 0:64], Atok.re("p (h j) -> p h j", h=2))
            yield
            ArbT, ArkT, XTs, Xs, AakTs = [], [], [], [], []
            for h in range(2):
                hp = slice(64 * h, 64 * h + 64)
                bm = P.bank()
                P.mm(bm[:, 0:128], bt[hp, ps], at[hp, ps])
                P.mm(bm[:, 128:256], at[hp, ps], bt[hp, ps])
                P.mm(bm[:, 256:384], kt[hp, ps], at[hp, ps])
                XT = self.sc("XTb%d" % h, (128, 128), BF16)
                X = self.sc("Xb%d" % h, (128, 128), BF16)
                AakT = self.sc("AakT%d" % h)
                P.tt(XT.v(), bm[:, 0:128], mst, ALU.mult)
                P.tt(X.v(), bm[:, 128:256], ms, ALU.mult)
                P.tt(AakT.v(), bm[:, 256:384], mst, ALU.mult)
                XTs.append(XT)
                Xs.append(X)
                AakTs.append(AakT)
            for h in range(2):
                hp = slice(64 * h, 64 * h + 64)
                bm2 = P.bank()
                P.mm(bm2[:, 0:128], bt[hp, ps], rt[hp, ps])
                P.mm(bm2[:, 128:256], kt[hp, ps], rt[hp, ps])
                a1 = self.sc("ArbT%d" % h)
                a2 = self.sc("ArkT%d" % h)
                P.tt(a1.v(), bm2[:, 0:128], mit, ALU.mult)
                P.tt(a2.v(), bm2[:, 128:256], mit, ALU.mult)
                ArbT.append(a1)
                ArkT.append(a2)
            yield
            for h in range(2):
                bz = P.bank()
                P.mm(bz[:, 0:128], AakTs[h].v(), Vtok)
                P.cp(Zc[:, h, 64:128], bz[:, 64 * h:64 * h + 64])
            Zb = self.sc("Zb", (128, 2, 128), BF16)
            P.cp(Zb.v(), Zc.v())
            for k in range(6):
                yield
                for h in range(2):
                    bzz = P.bank()
                    Zh = Zc[:, h, :]
                    P.mm(bzz[:, 0:128], XTs[h].v(), Zb[:, h, :])
                    P.tt(Zh, Zh, bzz[:, 0:128], ALU.add)
                    if k < 5:
                        P.cp(Zb[:, h, :], Zh)
                if k < 5:
                    for h in range(2):
                        bsqr = P.bank()
                        P.mm(bsqr[:, 0:128], Xs[h].v(), XTs[h].v())
                        if k < 4:
                            P.mm(bsqr[:, 128:256], XTs[h].v(), Xs[h].v())
                        P.act(XTs[h].v(), bsqr[:, 0:128], AF.Identity)
                        if k < 4:
                            P.cp(Xs[h].v(), bsqr[:, 128:256])
            yield
            AV = self.sc("AV", (128, 2, 128))
            P.cp(AV[:, 0, :].re("p (h j) -> p h j", h=2), Zc[:, :, 0:64])
            P.cp(AV[:, 1, :].re("p (h j) -> p h j", h=2), Zc[:, :, 64:128])
            Ahat = AV[:, 0, :]
            Vhat = AV[:, 1, :]
            rh0 = self.rh0[pr % 2]
            rh1 = self.rh1[pr % 2]
            byl = P.bank()
            for h in range(2):
                hp = slice(64 * h, 64 * h + 64)
                br = P.bank()
                P.mm(br[:, 0:128], Ahat, ArbT[h].v())
                P.tt(rh0[hp, 0:64], rt[hp, pr * 128:pr * 128 + 64], br[hp, 0:64], ALU.add)
                P.tt(rh1[hp, 64:128], rt[hp, pr * 128 + 64:pr * 128 + 128], br[hp, 64:128], ALU.add)
                P.mm(byl[:, 128 * h:128 * h + 128], ArbT[h].v(), Vhat, start=True, stop=False)
                P.mm(byl[:, 128 * h:128 * h + 128], ArkT[h].v(), Vtok, start=False, stop=True)
            Yloc = self.sc("Yloc")
            for h in range(2):
                P.act(Yloc[:, 64 * h:64 * h + 64], byl[:, 128 * h + 64 * h:128 * h + 64 * h + 64], AF.Identity)
            yield
            PTc, Qc = [], []
            for c in range(2):
                scr = slice(64 * c, 64 * c + 64)
                wl = W["t3"][:, pr * 128 + 64 * c + 63: pr * 128 + 64 * c + 64]
                bp_ = P.bank()
                P.mm(bp_[:, 0:128], Ahat[scr, :], Btok[scr, :])
                P.mm(bp_[:, 128:256], Btok[scr, :], Vhat[scr, :], start=True, stop=False)
                P.mm(bp_[:, 128:256], Ktok[scr, :], Vtok[scr, :], start=False, stop=True)
                pt_ = self.sc("PTc%d" % c)
                P.stt(pt_.v(), bp_[:, 0:128], 1.0, bd, ALU.mult, ALU.mult)
                P.tt(pt_.v(), pt_.v(), ident, ALU.add)
                qc_ = self.sc("Qc%d" % c)
                P.stt(qc_.v(), bp_[:, 128:256], wl, bd, ALU.mult, ALU.mult)
                PTc.append(pt_)
                Qc.append(qc_)
            yield
            bY = P.bank()
            for c in range(2):
                Scur = self.Sbd[self.sbd_i % 2]
                Snew = self.Sbd[(self.sbd_i + 1) % 2]
                wl = W["t3"][:, pr * 128 + 64 * c + 63: pr * 128 + 64 * c + 64]
                P.mm(bY[:, 0:128], (rh0 if c == 0 else rh1).v(), Scur.v(), start=(c == 0), stop=(c == 1))
                bS = P.bank()
                P.mm(bS[:, 0:128], PTc[c].v(), Scur.v())
                P.stt(Snew.v(), bS[:, 0:128], wl, Qc[c].v(), ALU.mult, ALU.add)
                self.sbd_i += 1
            y = self.sc("rwy")
            P.tt(y.v(), bY[:, 0:128], Yloc.v(), ALU.add)
            yield
            st = self.sc("gnst", (128, 16))
            y3 = y.v().re("p (h i) -> p h i", h=2)
            P.reduce(st[:, 0:2], y3, ALU.add)
            ysq = self.sc("ysq")
            P.tt(ysq.v(), y.v(), y.v(), ALU.mult)
            P.reduce(st[:, 2:4], ysq.v().re("p (h i) -> p h i", h=2), ALU.add)
            P.ts(st[:, 4:6], st[:, 0:2], 1.0 / 64, ALU.mult)
            P.tt(st[:, 6:8], st[:, 4:6], st[:, 4:6], ALU.mult)
            P.stt(st[:, 8:10], st[:, 2:4], 1.0 / 64, st[:, 6:8], ALU.mult, ALU.subtract)
            P.act(st[:, 10:12], st[:, 8:10], AF.Sqrt, bias=GN_EPS)
            P.recip(st[:, 12:14], st[:, 10:12])
            bb = P.bank()
            P.mm(bb[:, 0:128], rk[:, ps], bd)
            bon = self.sc("bon")
            P.tt(bon.v(), bb[:, 0:128], Vtok, ALU.mult)
            yn = self.sc("yn")
            for h in range(2):
                hc = slice(64 * h, 64 * h + 64)
                P.ts(yn[:, hc], y[:, hc], st[:, 4 + h:5 + h], ALU.subtract, st[:, 12 + h:13 + h], ALU.mult)
            P.tt(yn.v(), yn.v(), self.lnrow[:, 0:128], ALU.mult)
            P.tt(yn.v(), yn.v(), self.lnrow[:, 128:256], ALU.add)
            P.tt(yn.v(), yn.v(), bon.v(), ALU.add)
            bt_ = P.bank()
            P.tr(bt_[:, 0:128], yn.v(), ident)
            P.tt(yrw[:, ps], bt_[:, 0:128], W["sg"][:, ps], ALU.mult)
        P.dma(ST_ENG, self.ydst(2, tok0), yrw.v())
        if isinstance(self.y_loc, list) and (i % 2 == 1):
            ck = tok0 // 1024
            P.allgather(self.y_all[ck].v(), self.y_loc[ck].v(), GROUPS)


def build_A(S):
    P = Prog()
    import os
    if int(os.environ.get("KPAD", "0")):
        P.sb([128, int(os.environ["KPAD"]) * 256], F32, "padtile")
    P.make_banks(8)
    P.gen_banks = 4
    io = {}
    x = P.dram("x", [S, D], F32, kind="ExternalInput")
    io["wcat"] = P.dram("wcat", [D, NG * 128], F32, kind="ExternalInput")
    io["adaw"] = P.dram("adaw", [D, 2048], F32, kind="ExternalInput")
    io["pcol"] = P.dram("pcol", [128, NPC], F32, kind="ExternalInput")
    io["wuq"] = P.dram("wuq", [256, 384], F32, kind="ExternalInput")
    io["wukv"] = P.dram("wukv", [128, 256], F32, kind="ExternalInput")
    io["lora"] = P.dram("lora", [128, 256], F32, kind="ExternalInput")
    io["lnrow"] = P.dram("lnrow", [128, 256], F32, kind="ExternalInput")
    io["pos"] = P.dram("pos", [1, S], I32, kind="ExternalInput")
    io["cst"] = P.dram("cst", [128, NCST], F32, kind="ExternalInput")
    y_loc = P.dram("y_loc", [384, S], BF16, kind="ExternalOutput")
    ssq = P.dram("ssq", [1, S], F32, kind="ExternalOutput")
    PhaseA(P, S, x, y_loc, ssq, io).run()
    return P


def prep_B(inp, l, b, q, y_locs, ssqs, x_b, SB):
    f32 = np.float32
    t0, t1 = q * SB, (q + 1) * SB
    yT = np.stack([y_locs[g][br * 128:(br + 1) * 128, t0:t1] for g in range(4) for br in range(3)], axis=0)
    w_out = inp["w_out"][l]
    wout = np.stack([w_out[512 * br + 128 * g: 512 * br + 128 * g + 128, :] for g in range(4) for br in range(3)], axis=0)
    ssq = np.ascontiguousarray(np.concatenate([s[:, t0:t1] for s in ssqs], axis=0).T)
    return {
        "yT": np.ascontiguousarray(yT),
        "wout": np.ascontiguousarray(wout).astype(f32),
        "ssq4": ssq.astype(f32),
        "xin": np.ascontiguousarray(x_b[t0:t1]).astype(f32),
        "adawg": np.ascontiguousarray(inp["ada_w"][l][:, 2048:3072]).astype(f32),
        "cvec": _col8(inp["c"][b]),
        "gateb": np.ascontiguousarray(np.broadcast_to(inp["ada_b"][l][2048:3072], (128, 1024))).astype(f32),
        "fnw": np.ascontiguousarray(np.broadcast_to(inp["final_norm_w"], (128, 1024))).astype(f32),
        "onesb": np.ones((128, 128), f32),
    }


class PhaseB:
    def __init__(self, P, SB, io, xin, xo, xf, tag="B"):
        self.P, self.SB, self.io, self.xin, self.xo, self.xf, self.tag = P, SB, io, xin, xo, xf, tag

    def run(self):
        P, io, SB, tag = self.P, self.io, self.SB, self.tag
        sb = P.sb
        NT = SB // 128
        ones = sb([128, 128], F32, "onesB" + tag)
        P.dma("sp", ones.v(), io["onesb"].v())
        stg = [sb([128, 1024], F32, f"stgB{i}" + tag) for i in range(2)]
        w_sb = sb([128, 12, 1024], BF16, "woutsb" + tag)
        for c in range(12):
            s_ = stg[c % 2]
            P.dma("sp", s_.v(), io["wout"][c, :, :])
            P.cp(w_sb[:, c, :], s_.v(), eng="pool")
        cv = sb([128, 8], F32, "cvecB" + tag)
        P.dma("sp", cv.v(), io["cvec"].v())
        cact = sb([128, 8], F32, "cactB" + tag)
        P.act(cact.v(), cv.v(), AF.Silu)
        cbc = sb([128, 8, 128], F32, "cbcB" + tag)
        for kc in range(8):
            P.ts(cbc[:, kc, :], ones.v(), cact[:, kc:kc + 1], ALU.mult)
        gate = sb([128, 1024], F32, "gateB" + tag)
        P.dma("sp", gate.v(), io["gateb"].v())
        fnw = sb([128, 1024], F32, "fnwB" + tag)
        P.dma("sp", fnw.v(), io["fnw"].v())
        awv = io["adawg"].v().re("(c p) n -> p c n", p=128)
        for half in range(2):
            bk = P.bank()
            for kc in range(8):
                s_ = stg[kc % 2]
                P.dma("sp", s_[:, 0:512], awv[:, kc, half * 512:(half + 1) * 512])
                P.mm(bk[:, :], cbc[:, kc, :], s_[:, 0:512], start=(kc == 0), stop=(kc == 7))
            P.tt(gate[:, half * 512:(half + 1) * 512], gate[:, half * 512:(half + 1) * 512], bk[:, :], ALU.add)
        sq4 = sb([128, NT, 4], F32, "sq4B" + tag)
        P.dma("sp", sq4.v(), io["ssq4"].v().re("(n p) g -> p n g", p=128))
        rstd = sb([128, NT], F32, "rstdB" + tag)
        P.reduce(rstd.v(), sq4.v(), ALU.add)
        P.act(rstd.v(), rstd.v(), AF.Sqrt, scale=1.0 / 512, bias=EPS)
        P.recip(rstd.v(), rstd.v())
        yt = [sb([128, 12, 128], BF16, f"ytB{i}" + tag) for i in range(2)]
        xt = [sb([128, 1024], F32, f"xtB{i}" + tag) for i in range(2)]
        xn = [sb([128, 1024], F32, f"xnB{i}" + tag) for i in range(2)]
        t1 = sb([128, 512], F32, "t1B" + tag)
        xfo = sb([128, 1024], F32, "xfB" + tag)
        st = sb([128, 4], F32, "stB" + tag)
        yv = io["yT"].v()
        for t in range(NT):
            y_ = yt[t % 2]
            x_ = xt[t % 2]
            xn_ = xn[t % 2]
            P.dma("sp", y_.v(), yv[:, :, t * 128:(t + 1) * 128].re("c p n -> p c n"))
            P.dma("sp", x_.v(), self.xin[t * 128:(t + 1) * 128, :])
            for half in range(2):
                hs = slice(half * 512, (half + 1) * 512)
                bo = P.bank()
                bs = P.bank()
                no = ns = 0
                for c in range(12):
                    br = c % 3
                    if br == 1:
                        P.mm(bs[:, :], y_[:, c, :], w_sb[:, c, hs], start=(ns == 0), stop=(ns == 3))
                        ns += 1
                    else:
                        P.mm(bo[:, :], y_[:, c, :], w_sb[:, c, hs], start=(no == 0), stop=(no == 7))
                        no += 1
                P.act(t1.v(), bs[:, :], AF.Identity, scale=rstd[:, t:t + 1])
                P.tt(t1.v(), t1.v(), bo[:, :], ALU.add)
                P.tt(t1.v(), t1.v(), gate[:, hs], ALU.mult)
                P.tt(xn_[:, hs], t1.v(), x_[:, hs], ALU.add, eng="pool")
            P.dma("sp", self.xo[t * 128:(t + 1) * 128, :], xn_.v())
            P.act(xfo.v(), xn_.v(), AF.Square, accum=st[:, 0:1])
            P.act(st[:, 1:2], st[:, 0:1], AF.Sqrt, scale=1.0 / D, bias=EPS)
            P.recip(st[:, 2:3], st[:, 1:2])
            P.stt(xfo.v(), xn_.v(), st[:, 2:3], fnw.v(), ALU.mult, ALU.mult)
            P.dma("sp", self.xf[t * 128:(t + 1) * 128, :], xfo.v())


def build_B(SB):
    P = Prog()
    P.make_banks(8)
    io = {}
    io["yT"] = P.dram("yT", [12, 128, SB], BF16, kind="ExternalInput")
    io["wout"] = P.dram("wout", [12, 128, 1024], F32, kind="ExternalInput")
    io["ssq4"] = P.dram("ssq4", [SB, 4], F32, kind="ExternalInput")
    xin = P.dram("xin", [SB, D], F32, kind="ExternalInput")
    io["adawg"] = P.dram("adawg", [D, 1024], F32, kind="ExternalInput")
    io["cvec"] = P.dram("cvec", [128, 8], F32, kind="ExternalInput")
    io["gateb"] = P.dram("gateb", [128, 1024], F32, kind="ExternalInput")
    io["fnw"] = P.dram("fnw", [128, 1024], F32, kind="ExternalInput")
    io["onesb"] = P.dram("onesb", [128, 128], F32, kind="ExternalInput")
    xo = P.dram("xo", [SB, D], F32, kind="ExternalOutput")
    xf = P.dram("xf", [SB, D], F32, kind="ExternalOutput")
    PhaseB(P, SB, io, xin, xo, xf).run()
    return P


_CACHE = {}


def _prog(kind, n):
    key = (kind, n)
    if key not in _CACHE:
        P = build_A(n) if kind == "A" else build_B(n)
        _CACHE[key] = P.finish()
    return _CACHE[key]


def kernel_unfused(**inputs):
    inp = {k: np.asarray(v) for k, v in inputs.items()}
    B, S, _ = inp["x"].shape
    SB = S // 4
    ncA = _prog("A", S)
    ncB = _prog("B", SB)
    x_cur = [np.ascontiguousarray(inp["x"][b]).astype(np.float32) for b in range(B)]
    out = None
    for l in range(2):
        insA = []
        for b in range(B):
            for g in range(4):
                d = prep_A(inp, l, b, g, S)
                d["x"] = x_cur[b]
                insA.append(d)
        resA = run_bass_kernel_spmd(ncA, insA, core_ids=list(range(8))).results
        insB = []
        for b in range(B):
            ys = [np.asarray(resA[b * 4 + g]["y_loc"]) for g in range(4)]
            sq = [np.asarray(resA[b * 4 + g]["ssq"]) for g in range(4)]
            for q in range(4):
                insB.append(prep_B(inp, l, b, q, ys, sq, x_cur[b], SB))
        resB = run_bass_kernel_spmd(ncB, insB, core_ids=list(range(8))).results
        x_cur = [np.concatenate([np.asarray(resB[b * 4 + q]["xo"]) for q in range(4)], axis=0) for b in range(B)]
        if l == 1:
            out = np.stack([np.concatenate([np.asarray(resB[b * 4 + q]["xf"]) for q in range(4)], axis=0)
                            for b in range(B)], axis=0)
    return out.astype(np.float32)


GROUPS = [[0, 1, 2, 3], [4, 5, 6, 7]]


def prep_F(inp, b, g, S):
    f32 = np.float32
    d = {"x": np.ascontiguousarray(inp["x"][b]).astype(f32)}
    for l in range(2):
        a = prep_A(inp, l, b, g, S)
        for k in ("wcat", "adaw", "pcol", "wuq", "wukv", "lora", "lnrow"):
            d[f"{k}{l}"] = a[k]
        if l == 0:
            d["pos"] = a["pos"]
            d["cst"] = a["cst"]
        w_out = inp["w_out"][l]
        d[f"wout{l}"] = np.ascontiguousarray(
            np.stack([w_out[512 * br + 128 * g: 512 * br + 128 * g + 128, :] for br in range(3)], axis=0)).astype(f32)
        d[f"adawg{l}"] = np.ascontiguousarray(inp["ada_w"][l][:, 2048:3072]).astype(f32)
        d[f"gateb{l}"] = np.ascontiguousarray(np.broadcast_to(inp["ada_b"][l][2048:3072], (128, 1024))).astype(f32)
    d["fnw"] = np.ascontiguousarray(np.broadcast_to(inp["final_norm_w"], (128, 1024))).astype(f32)
    return d


class PhaseBC:
    def __init__(self, P, A, S):
        self.P, self.A, self.S = P, A, S
        self.rstd = P.sb([128, S // 128], F32, "rstdBC")
        self.sq64 = P.sb([S // 128, 128], F32, "sq64BC")
        self.st = P.sb([128, 4], F32, "stBC")

    def run(self, l, last, io, x_src, y_loc, ssq_loc, ssq_sum, part_loc, part_sum, xcur, out):
        P, A, S = self.P, self.A, self.S
        W, G = A.W, A.G
        NT = S // 128
        ident = A.c("ident")
        ones = A.c("ones")
        P.allreduce(ssq_sum.v(), ssq_loc.v(), GROUPS)
        P.dma("sp", self.sq64.v(), ssq_sum.v().re("o (n p) -> (o n) p", p=128))
        bk = P.bank()
        P.tr(bk[:, 0:NT], self.sq64.v(), ident[0:NT, 0:NT])
        P.act(self.rstd.v(), bk[:, 0:NT], AF.Sqrt, scale=1.0 / 512, bias=EPS)
        P.recip(self.rstd.v(), self.rstd.v())
        wo = [View(G[c].h[:].bitcast(BF16), G[c].d) for c in range(3)]
        for c in range(3):
            s_ = A.xt[c % 2]
            P.dma("sp", s_.v(), io["wout"][c, :, :])
            P.cp(wo[c], s_.v(), eng="pool")
        ytl = [View(W["t0"].h[:].bitcast(BF16), W["t0"].d), View(W["t1"].h[:].bitcast(BF16), W["t1"].d)]
        hT32 = View(W["hT"].h[:].rearrange("p a b -> p (a b)").bitcast(F32), W["hT"].d)
        hTa = View(hT32.ap[:, 0:1024], Dep("hTa"))
        hTb = View(hT32.ap[:, 1024:2048], Dep("hTb"))
        for v_ in (hTa, hTb):
            v_.dep.lw = W["hT"].d.lw
            v_.dep.rd = dict(W["hT"].d.rd)
        parts = [A.xn[0].v(), hTa]
        tmp = W["t2"]
        yv = y_loc.v().re("(c p) s -> p c s", p=128)
        for t in range(NT):
            yt = ytl[t % 2]
            ytv = yt[:, 0:384].re("p (c s) -> p c s", c=3)
            P.dma("sp", ytv, yv[:, :, t * 128:(t + 1) * 128])
            for half in range(2):
                hs = slice(half * 512, (half + 1) * 512)
                bo = P.bank()
                bs = P.bank()
                P.mm(bo[:, :], ytv[:, 0, :], wo[0][:, hs], start=True, stop=False)
                P.mm(bo[:, :], ytv[:, 2, :], wo[2][:, hs], start=False, stop=True)
                P.mm(bs[:, :], ytv[:, 1, :], wo[1][:, hs], start=True, stop=True)
                part = parts[t % 2]
                P.act(tmp.v(), bs[:, :], AF.Identity, scale=self.rstd[:, t:t + 1])
                P.tt(part[:, hs], tmp.v(), bo[:, :], ALU.add)
            ck, r0 = t // 8, (t % 8) * 128
            P.dma("sp", part_loc[ck][r0:r0 + 128, :], parts[t % 2], semdep=parts[t % 2].dep)
            if t % 8 == 7:
                P.allreduce(part_sum[ck].v(), part_loc[ck].v(), GROUPS)
        cbc = hTa.re("p (k m) -> p k m", k=8)
        gate = hTb
        for kc in range(8):
            P.ts(cbc[:, kc, :], ones, A.cact[:, kc:kc + 1], ALU.mult)
        P.dma("sp", gate, io["gateb"].v())
        awv = io["adawg"].v().re("(c p) n -> p c n", p=128)
        stg = [W["t3"], W["t4"]]
        for half in range(2):
            bk = P.bank()
            for kc in range(8):
                s_ = stg[kc % 2]
                P.dma("sp", s_.v(), awv[:, kc, half * 512:(half + 1) * 512])
                P.mm(bk[:, :], cbc[:, kc, :], s_.v(), start=(kc == 0), stop=(kc == 7))
            P.tt(gate[:, half * 512:(half + 1) * 512], gate[:, half * 512:(half + 1) * 512], bk[:, :], ALU.add)
        fnw = A.xn[0]
        if last:
            P.dma("sp", fnw.v(), io["fnw"].v())
        XH = [G[k] for k in range(8)]
        PH = [G[8 + k] for k in range(7)] + [W["t5"]]
        st = self.st
        n = 0
        for t in range(NT):
            rows = slice(t * 128, (t + 1) * 128)
            pr0 = (t % 8) * 128
            xs_, ps_ = [], []
            for half in range(2):
                hs = slice(half * 512, (half + 1) * 512)
                xh = XH[n % 8]
                ph = PH[n % 8]
                n += 1
                P.dma("sp", xh.v(), x_src[rows, hs])
                P.dma("sp", ph.v(), part_sum[t // 8][pr0:pr0 + 128, hs])
                P.tt(ph.v(), ph.v(), gate[:, hs], ALU.mult)
                P.tt(xh.v(), xh.v(), ph.v(), ALU.add)
                if not last:
                    P.dma("sp", xcur[rows, hs], xh.v(), semdep=xh.d)
                else:
                    P.act(ph.v(), xh.v(), AF.Square, accum=st[:, half:half + 1])
                xs_.append(xh)
                ps_.append(ph)
            if last:
                P.tt(st[:, 2:3], st[:, 0:1], st[:, 1:2], ALU.add)
                P.act(st[:, 3:4], st[:, 2:3], AF.Sqrt, scale=1.0 / D, bias=EPS)
                P.recip(st[:, 3:4], st[:, 3:4])
                for half in range(2):
                    hs = slice(half * 512, (half + 1) * 512)
                    P.stt(ps_[half].v(), xs_[half].v(), st[:, 3:4], fnw[:, hs], ALU.mult, ALU.mult)
                    P.dma("sp", out[rows, hs], ps_[half].v(), semdep=ps_[half].d)
        for v_ in (hTa, hTb):
            for k_, val in list(v_.dep.rd.items()) + ([v_.dep.lw] if v_.dep.lw else []):
                if W["hT"].d.rd.get(k_, 0) < val:
                    W["hT"].d.rd[k_] = val


def build_fused(S):
    P = Prog()
    P.make_banks(8)
    P.gen_banks = 4
    x0 = P.dram("x", [S, D], F32, kind="ExternalInput")
    out = P.dram("out", [S, D], F32, kind="ExternalOutput")
    xcur = P.dram("xcur", [S, D], F32)
    shared = {"pos": P.dram("pos", [1, S], I32, kind="ExternalInput"),
              "cst": P.dram("cst", [128, NCST], F32, kind="ExternalInput")}
    fnw = P.dram("fnw", [128, 1024], F32, kind="ExternalInput")
    A = None
    BC = None
    for l in range(2):
        io = dict(shared)
        io["wcat"] = P.dram(f"wcat{l}", [D, NG * 128], F32, kind="ExternalInput")
        io["adaw"] = P.dram(f"adaw{l}", [D, 2048], F32, kind="ExternalInput")
        io["pcol"] = P.dram(f"pcol{l}", [128, NPC], F32, kind="ExternalInput")
        io["wuq"] = P.dram(f"wuq{l}", [256, 384], F32, kind="ExternalInput")
        io["wukv"] = P.dram(f"wukv{l}", [128, 256], F32, kind="ExternalInput")
        io["lora"] = P.dram(f"lora{l}", [128, 256], F32, kind="ExternalInput")
        io["lnrow"] = P.dram(f"lnrow{l}", [128, 256], F32, kind="ExternalInput")
        io["wout"] = P.dram(f"wout{l}", [3, 128, 1024], F32, kind="ExternalInput")
        io["adawg"] = P.dram(f"adawg{l}", [D, 1024], F32, kind="ExternalInput")
        io["gateb"] = P.dram(f"gateb{l}", [128, 1024], F32, kind="ExternalInput")
        io["fnw"] = fnw
        y_loc = P.dram(f"y_loc{l}", [384, S], BF16)
        ssq_loc = P.dram(f"ssq_loc{l}", [1, S], F32)
        ssq_sum = P.dram(f"ssq_sum{l}", [1, S], F32)
        part_loc = [P.dram(f"part_loc{l}_{k}", [1024, D], F32) for k in range(S // 1024)]
        part_sum = [P.dram(f"part_sum{l}_{k}", [1024, D], F32) for k in range(S // 1024)]
        x_src = x0 if l == 0 else xcur
        if A is None:
            A = PhaseA(P, S, x_src, y_loc, ssq_loc, io)
        else:
            A.x, A.y_loc, A.ssq_out, A.io = x_src, y_loc, ssq_loc, io
        A.run()
        if BC is None:
            BC = PhaseBC(P, A, S)
        BC.run(l, l == 1, io, x_src, y_loc, ssq_loc, ssq_sum, part_loc, part_sum, xcur, out)
    return P


def kernel(**inputs):
    inp = {k: np.asarray(v) for k, v in inputs.items()}
    B, S, _ = inp["x"].shape
    key = ("F2", S)
    if key not in _CACHE:
        _CACHE[key] = build_fused2(S).finish()
    nc = _CACHE[key]
    ins = [prep_F2(inp, b, g, S) for b in range(B) for g in range(4)]
    res = run_bass_kernel_spmd(nc, ins, core_ids=list(range(8))).results
    return np.stack([np.asarray(res[4 * b]["out"]) for b in range(B)], axis=0).astype(np.float32)


class PhaseD:
    def __init__(self, P, A, S):
        self.P, self.A, self.S = P, A, S
        NT = S // 128
        self.rstd = P.sb([128, NT], F32, "rstdD")
        self.sq64 = P.sb([NT, 4, 128], F32, "sq64D")
        self.st = P.sb([128, 4], F32, "stD")

    def run(self, l, last, io, x_src, y_all, ssq_all, xcur, out):
        P, A, S = self.P, self.A, self.S
        W, G = A.W, A.G
        NT = S // 128
        ident = A.c("ident")
        ones = A.c("ones")
        sq = self.sq64
        P.dma("sp", sq.v(), ssq_all.v().re("g (n p) -> n g p", p=128))
        P.tt(sq[:, 0, :], sq[:, 0, :], sq[:, 1, :], ALU.add)
        P.tt(sq[:, 2, :], sq[:, 2, :], sq[:, 3, :], ALU.add)
        P.tt(sq[:, 0, :], sq[:, 0, :], sq[:, 2, :], ALU.add)
        bk = P.bank()
        P.tr(bk[:, 0:NT], sq[:, 0, :], ident[0:NT, 0:NT])
        P.act(self.rstd.v(), bk[:, 0:NT], AF.Sqrt, scale=1.0 / 512, bias=EPS)
        P.recip(self.rstd.v(), self.rstd.v())
        wo = [View(G[c].h[:].bitcast(BF16), G[c].d) for c in range(12)]
        for c in range(12):
            s_ = A.xt[c % 2]
            P.dma("sp", s_.v(), io["woutf"][c, :, :])
            P.cp(wo[c], s_.v(), eng=("dve", "act")[c % 2])
        hT32 = View(W["hT"].h[:].rearrange("p a b -> p (a b)").bitcast(F32), W["hT"].d)
        hTa = View(hT32.ap[:, 0:1024], Dep("hTa"))
        hTb = View(hT32.ap[:, 1024:2048], Dep("hTb"))
        for v_ in (hTa, hTb):
            v_.dep.lw = W["hT"].d.lw
            v_.dep.rd = dict(W["hT"].d.rd)
        cbc = hTa.re("p (k m) -> p k m", k=8)
        gate = hTb
        for kc in range(8):
            P.ts(cbc[:, kc, :], ones, A.cact[:, kc:kc + 1], ALU.mult)
        P.dma("sp", gate, io["gateb"].v())
        awv = io["adawg"].v().re("(c p) n -> p c n", p=128)
        stg = [W["t4"], W["t5"]]
        for half in range(2):
            bk = P.bank()
            for kc in range(8):
                s_ = stg[kc % 2]
                P.dma("sp", s_.v(), awv[:, kc, half * 512:(half + 1) * 512])
                P.mm(bk[:, :], cbc[:, kc, :], s_.v(), start=(kc == 0), stop=(kc == 7))
            P.tt(gate[:, half * 512:(half + 1) * 512], gate[:, half * 512:(half + 1) * 512], bk[:, :], ALU.add)
        fnw = hTa
        if last:
            P.dma("sp", fnw, io["fnw"].v())
        ybuf = [(View(W["t0"].h[:].bitcast(BF16), W["t0"].d), View(W["t1"].h[:].bitcast(BF16), W["t1"].d)),
                (View(W["t2"].h[:].bitcast(BF16), W["t2"].d), View(W["t3"].h[:].bitcast(BF16), W["t3"].d))]
        OH = [G[12], G[13], G[14], W["t4"], W["t5"]]
        tmp = W["ssqr"]
        st = self.st
        n = 0
        for t in range(NT):
            rows = slice(t * 128, (t + 1) * 128)
            ck, c0 = t // 8, (t % 8) * 128
            ya, yb = ybuf[t % 2]
            ysrc = y_all[ck].v().re("(c p) s -> p c s", p=128)
            P.dma("sp", ya.re("p (c s) -> p c s", c=8), ysrc[:, 0:8, c0:c0 + 128])
            P.dma("sp", yb[:, 0:512].re("p (c s) -> p c s", c=4), ysrc[:, 8:12, c0:c0 + 128])
            x_ = A.xt[t % 2]
            P.dma("sp", x_.v(), x_src[rows, :])

            def ych(c):
                return ya[:, c * 128:(c + 1) * 128] if c < 8 else yb[:, (c - 8) * 128:(c - 7) * 128]

            ohs = []
            for half in range(2):
                hs = slice(half * 512, (half + 1) * 512)
                bo = P.bank()
                bs = P.bank()
                no = ns = 0
                for c in range(12):
                    if c % 3 == 1:
                        P.mm(bs[:, :], ych(c), wo[c][:, hs], start=(ns == 0), stop=(ns == 3))
                        ns += 1
                    else:
                        P.mm(bo[:, :], ych(c), wo[c][:, hs], start=(no == 0), stop=(no == 7))
                        no += 1
                oh = OH[n % 5]
                n += 1
                P.act(oh.v(), bs[:, :], AF.Identity, scale=self.rstd[:, t:t + 1])
                P.tt(oh.v(), oh.v(), bo[:, :], ALU.add)
                P.tt(oh.v(), oh.v(), gate[:, hs], ALU.mult)
                P.tt(oh.v(), oh.v(), x_[:, hs], ALU.add)
                if not last:
                    P.dma(ST_ENG, xcur[rows, hs], oh.v(), semdep=oh.d)
                else:
                    P.act(A.xn[0][:, hs], oh.v(), AF.Square, accum=st[:, half:half + 1])
                ohs.append(oh)
            if last:
                P.tt(st[:, 2:3], st[:, 0:1], st[:, 1:2], ALU.add)
                P.act(st[:, 3:4], st[:, 2:3], AF.Sqrt, scale=1.0 / D, bias=EPS)
                P.recip(st[:, 3:4], st[:, 3:4])
                for half in range(2):
                    hs = slice(half * 512, (half + 1) * 512)
                    P.stt(ohs[half].v(), ohs[half].v(), st[:, 3:4], fnw[:, hs], ALU.mult, ALU.mult)
                    P.dma(ST_ENG, out[rows, hs], ohs[half].v(), semdep=ohs[half].d)
        for v_ in (hTa, hTb):
            for k_, val in list(v_.dep.rd.items()) + ([v_.dep.lw] if v_.dep.lw else []):
                if W["hT"].d.rd.get(k_, 0) < val:
                    W["hT"].d.rd[k_] = val


def prep_F2(inp, b, g, S):
    d = prep_F(inp, b, g, S)
    for l in range(2):
        w_out = inp["w_out"][l]
        d[f"woutf{l}"] = np.ascontiguousarray(
            np.stack([w_out[512 * br + 128 * gg: 512 * br + 128 * gg + 128, :] for gg in range(4) for br in range(3)],
                     axis=0)).astype(np.float32)
        del d[f"wout{l}"]
    return d


def build_fused2(S):
    P = Prog()
    P.make_banks(8)
    P.gen_banks = 4
    x0 = P.dram("x", [S, D], F32, kind="ExternalInput")
    out = P.dram("out", [S, D], F32, kind="ExternalOutput")
    xcur = P.dram("xcur", [S, D], F32)
    shared = {"pos": P.dram("pos", [1, S], I32, kind="ExternalInput"),
              "cst": P.dram("cst", [128, NCST], F32, kind="ExternalInput")}
    fnw = P.dram("fnw", [128, 1024], F32, kind="ExternalInput")
    rope_cache = P.dram("ropec", [2, 128, S], F32)
    A = None
    Dp = None
    NCK = S // 1024
    for l in range(2):
        io = dict(shared)
        io["wcat"] = P.dram(f"wcat{l}", [D, NG * 128], F32, kind="ExternalInput")
        io["adaw"] = P.dram(f"adaw{l}", [D, 2048], F32, kind="ExternalInput")
        io["pcol"] = P.dram(f"pcol{l}", [128, NPC], F32, kind="ExternalInput")
        io["wuq"] = P.dram(f"wuq{l}", [256, 384], F32, kind="ExternalInput")
        io["wukv"] = P.dram(f"wukv{l}", [128, 256], F32, kind="ExternalInput")
        io["lora"] = P.dram(f"lora{l}", [128, 256], F32, kind="ExternalInput")
        io["lnrow"] = P.dram(f"lnrow{l}", [128, 256], F32, kind="ExternalInput")
        io["woutf"] = P.dram(f"woutf{l}", [12, 128, 1024], F32, kind="ExternalInput")
        io["adawg"] = P.dram(f"adawg{l}", [D, 1024], F32, kind="ExternalInput")
        io["gateb"] = P.dram(f"gateb{l}", [128, 1024], F32, kind="ExternalInput")
        io["fnw"] = fnw
        y_loc = [P.dram(f"y_loc{l}_{k}", [384, 1024], BF16) for k in range(NCK)]
        y_all = [P.dram(f"y_all{l}_{k}", [4 * 384, 1024], BF16) for k in range(NCK)]
        ssq_loc = P.dram(f"ssq_loc{l}", [1, S], F32)
        ssq_all = P.dram(f"ssq_all{l}", [4, S], F32)
        x_src = x0 if l == 0 else xcur
        if A is None:
            A = PhaseA(P, S, x_src, y_loc, ssq_loc, io)
        else:
            A.x, A.y_loc, A.ssq_out, A.io = x_src, y_loc, ssq_loc, io
        A.y_all = y_all
        A.rope_cache = rope_cache
        A.layer = l
        A.run()
        P.allgather(ssq_all.v(), ssq_loc.v(), GROUPS)
        if Dp is None:
            Dp = PhaseD(P, A, S)
        Dp.run(l, l == 1, io, x_src, y_all, ssq_all, xcur, out)
    return P
```
